# Optimizing a Trainium2 kernel written in Bass

```python
import math
import jax, jax.numpy as jnp
from jax import lax
import numpy as np

D_MODEL = 1024
BATCH = 4
SEQ = 4096
DEPTH = 4

N_MIXERS = 3
EPS = 1e-6
N_POOL_LAYERS = (DEPTH + 2) // 3
N_GDN_LAYERS = (DEPTH + 1) // 3
N_MLA_LAYERS = DEPTH // 3

POOL_WIDTH = 2 * D_MODEL
POOL_WINDOWS = (2, 4, 8, 16)
POOL_GROUP = POOL_WIDTH // len(POOL_WINDOWS)

GDN_HEADS = 8
GDN_DK = 128
GDN_DV = 256
GDN_CONV = 4
GDN_CHUNK = 64
GDN_QK = GDN_HEADS * GDN_DK
GDN_V = GDN_HEADS * GDN_DV
GDN_CONV_CH = 2 * GDN_QK + GDN_V
GDN_IN = 2 * GDN_QK + 2 * GDN_V + 2 * GDN_HEADS

MLA_HEADS = 16
MLA_NOPE = 128
MLA_ROPE = 64
MLA_V = 128
MLA_Q_LORA = 768
MLA_KV_LORA = 512
MLA_QK = MLA_NOPE + MLA_ROPE
MLA_WIDTH = MLA_HEADS * MLA_V
MLA_IN = MLA_Q_LORA + MLA_KV_LORA + MLA_ROPE + MLA_WIDTH
ROPE_THETA = 10000.0
Q_BLOCK = 128

kernel_name = "hybrid_pool_gdn_mla_trunk"


def rmsnorm(x, g):
    xf = x.astype(jnp.float32)
    y = xf * lax.rsqrt(jnp.mean(xf * xf, axis=-1, keepdims=True) + EPS)
    return (y * g.astype(jnp.float32)).astype(x.dtype)


def l2norm(x):
    return x * lax.rsqrt(jnp.sum(x * x, axis=-1, keepdims=True) + EPS)


def pool_mixer(h, w_in, w_grp, scale, w_out):
    B, S, _ = h.shape
    u, gate = jnp.split(h @ w_in, 2, axis=-1)
    c = jnp.pad(jnp.cumsum(u.astype(jnp.float32), axis=1), ((0, 0), (1, 0), (0, 0)))
    t = jnp.arange(S)
    groups = []
    for gi, w in enumerate(POOL_WINDOWS):
        sl = slice(gi * POOL_GROUP, (gi + 1) * POOL_GROUP)
        cg = c[:, :, sl]
        lo = jnp.pad(cg[:, :S - w + 1], ((0, 0), (w - 1, 0), (0, 0)))
        cnt = jnp.minimum(t + 1, w).astype(jnp.float32)[None, :, None]
        mean = (cg[:, 1:] - lo) / cnt
        groups.append(mean.astype(u.dtype) - u[..., sl])
    p = jnp.stack(groups, axis=2)
    p = jnp.einsum('bsgi,gio->bsgo', p, w_grp).reshape(B, S, POOL_WIDTH)
    y = p * scale * jax.nn.silu(gate)
    return y @ w_out


def causal_depthwise_conv(u, w):
    K, C = w.shape
    return lax.conv_general_dilated(u, w[:, None, :], window_strides=(1,),
                                    padding=[(K - 1, 0)],
                                    dimension_numbers=('NWC', 'WIO', 'NWC'),
                                    feature_group_count=C)


def chunk_gated_delta_rule(q, k, v, g, beta):
    B, S, H, dk = q.shape
    dv = v.shape[-1]
    C = GDN_CHUNK
    N = S // C

    def to_chunks(a):
        return a.reshape(B, N, C, H, -1).transpose(0, 3, 1, 2, 4)

    q, k, v = to_chunks(q), to_chunks(k), to_chunks(v)
    beta = beta.reshape(B, N, C, H).transpose(0, 3, 1, 2)
    g = jnp.cumsum(g.reshape(B, N, C, H).transpose(0, 3, 1, 2), axis=-1)
    k_beta = k * beta[..., None]
    v_beta = v * beta[..., None]
    causal = jnp.tril(jnp.ones((C, C), dtype=bool))
    strict = jnp.tril(jnp.ones((C, C), dtype=bool), -1)
    diff = g[..., :, None] - g[..., None, :]
    decay = jnp.exp(jnp.where(causal, diff, -jnp.inf))
    L = jnp.where(strict, jnp.einsum('bhnid,bhnjd->bhnij', k_beta, k) * decay, 0.0)
    A = jnp.eye(C, dtype=L.dtype) + L
    rhs = jnp.concatenate([v_beta, k_beta * jnp.exp(g)[..., None]], axis=-1)
    sol = lax.linalg.triangular_solve(A, rhs, left_side=True, lower=True)
    u_ps = sol[..., :dv]
    w_cd = sol[..., dv:]
    attn = jnp.where(causal, jnp.einsum('bhnid,bhnjd->bhnij', q, k) * decay, 0.0)

    def step(state, xs):
        q_c, k_c, u_c, w_c, g_c, a_c = xs
        v_new = u_c - jnp.einsum('bhck,bhkv->bhcv', w_c, state)
        o = (jnp.einsum('bhck,bhkv->bhcv', q_c * jnp.exp(g_c)[..., None], state)
             + jnp.einsum('bhij,bhjv->bhiv', a_c, v_new))
        g_last = g_c[..., -1]
        k_dec = k_c * jnp.exp(g_last[..., None] - g_c)[..., None]
        state = state * jnp.exp(g_last)[..., None, None] + jnp.einsum('bhck,bhcv->bhkv', k_dec, v_new)
        return state, o

    mv = lambda a: jnp.moveaxis(a, 2, 0)
    xs = (mv(q), mv(k), mv(u_ps), mv(w_cd), mv(g), mv(attn))
    state0 = jnp.zeros((B, H, dk, dv), jnp.float32)
    _, o = lax.scan(step, state0, xs)
    return o.transpose(1, 0, 3, 2, 4).reshape(B, S, H, dv)


def gdn_mixer(h, w_in, conv_w, a_log, dt_bias, norm_g, w_out):
    B, S, _ = h.shape
    f32 = jnp.float32
    proj = h @ w_in
    qkv, gate, b_raw, a_raw = jnp.split(
        proj, [GDN_CONV_CH, GDN_CONV_CH + GDN_V, GDN_CONV_CH + GDN_V + GDN_HEADS], axis=-1)
    qkv = jax.nn.silu(causal_depthwise_conv(qkv, conv_w))
    q, k, v = jnp.split(qkv, [GDN_QK, 2 * GDN_QK], axis=-1)
    q = l2norm(q.reshape(B, S, GDN_HEADS, GDN_DK).astype(f32)) * (GDN_DK ** -0.5)
    k = l2norm(k.reshape(B, S, GDN_HEADS, GDN_DK).astype(f32))
    v = v.reshape(B, S, GDN_HEADS, GDN_DV).astype(f32)
    beta = jax.nn.sigmoid(b_raw.astype(f32))
    g = -jnp.exp(a_log.astype(f32)) * jax.nn.softplus(a_raw.astype(f32) + dt_bias.astype(f32))
    o = chunk_gated_delta_rule(q, k, v, g, beta)
    o = rmsnorm(o, norm_g) * jax.nn.silu(gate.reshape(B, S, GDN_HEADS, GDN_DV).astype(f32))
    return o.reshape(B, S, GDN_V).astype(h.dtype) @ w_out


def rope(x, pos):
    half = x.shape[-1] // 2
    inv = ROPE_THETA ** (-jnp.arange(half, dtype=jnp.float32) / half)
    ang = pos.astype(jnp.float32)[..., None, None] * inv
    cos, sin = jnp.cos(ang), jnp.sin(ang)
    x1, x2 = x[..., :half], x[..., half:]
    return jnp.concatenate([x1 * cos - x2 * sin, x1 * sin + x2 * cos], axis=-1).astype(x.dtype)


def causal_block_attention(q, k, v):
    B, S, H, Dq = q.shape
    nb = S // Q_BLOCK
    scale = Dq ** -0.5
    qb = q.reshape(B, nb, Q_BLOCK, H, Dq).transpose(1, 0, 2, 3, 4)
    kpos = jnp.arange(S)

    def block(args):
        i, q_blk = args
        s = jnp.einsum('bqhd,bkhd->bhqk', q_blk, k).astype(jnp.float32) * scale
        qpos = i * Q_BLOCK + jnp.arange(Q_BLOCK)
        s = jnp.where(kpos[None, :] <= qpos[:, None], s, -jnp.inf)
        p = jax.nn.softmax(s, axis=-1)
        return jnp.einsum('bhqk,bkhd->bqhd', p.astype(v.dtype), v)

    o = lax.map(block, (jnp.arange(nb), qb))
    return o.transpose(1, 0, 2, 3, 4).reshape(B, S, H, v.shape[-1])


def mla_mixer(h, pos, w_in, q_norm_g, w_uq, kv_norm_g, w_ukv, w_out):
    B, S, _ = h.shape
    proj = h @ w_in
    cq, ckv, k_rope, gate = jnp.split(
        proj, [MLA_Q_LORA, MLA_Q_LORA + MLA_KV_LORA, MLA_Q_LORA + MLA_KV_LORA + MLA_ROPE], axis=-1)
    q = (rmsnorm(cq, q_norm_g) @ w_uq).reshape(B, S, MLA_HEADS, MLA_QK)
    kv = (rmsnorm(ckv, kv_norm_g) @ w_ukv).reshape(B, S, MLA_HEADS, MLA_NOPE + MLA_V)
    q_nope, q_rope = q[..., :MLA_NOPE], q[..., MLA_NOPE:]
    k_nope, v = kv[..., :MLA_NOPE], kv[..., MLA_NOPE:]
    q_rope = rope(q_rope, pos)
    k_rope = rope(k_rope[:, :, None, :], pos)
    q = jnp.concatenate([q_nope, q_rope], axis=-1)
    k = jnp.concatenate([k_nope, jnp.broadcast_to(k_rope, (B, S, MLA_HEADS, MLA_ROPE))], axis=-1)
    o = causal_block_attention(q, k, v).reshape(B, S, MLA_WIDTH)
    return (o * jax.nn.silu(gate)) @ w_out


def setup_inputs(seed: int = 0) -> dict:
    key = jax.random.key(seed)
    ks = jax.random.split(key, 24)
    nrm = lambda k, shape, fan: jax.random.normal(k, shape, jnp.float32) * (fan ** -0.5)
    gain = lambda k, shape: 1.0 + 0.02 * jax.random.normal(k, shape, jnp.float32)
    nA, nB, nC = N_POOL_LAYERS, N_GDN_LAYERS, N_MLA_LAYERS
    x = jax.random.normal(ks[0], (BATCH, SEQ, D_MODEL), jnp.float32)
    positions = jnp.broadcast_to(jnp.arange(SEQ, dtype=jnp.int32)[None, :], (BATCH, SEQ))
    dt = jnp.exp(jax.random.uniform(ks[10], (nB, GDN_HEADS), jnp.float32,
                                    math.log(1e-3), math.log(1e-1)))
    return {
        "x": x,
        "positions": positions,
        "norm_g": gain(ks[1], (DEPTH, D_MODEL)),
        "pool_w_in": nrm(ks[2], (nA, D_MODEL, 2 * POOL_WIDTH), D_MODEL),
        "pool_w_grp": nrm(ks[3], (nA, len(POOL_WINDOWS), POOL_GROUP, POOL_GROUP), POOL_GROUP),
        "pool_scale": gain(ks[4], (nA, POOL_WIDTH)),
        "pool_w_out": nrm(ks[5], (nA, POOL_WIDTH, D_MODEL), POOL_WIDTH),
        "gdn_w_in": nrm(ks[6], (nB, D_MODEL, GDN_IN), D_MODEL),
        "gdn_conv": nrm(ks[7], (nB, GDN_CONV, GDN_CONV_CH), GDN_CONV),
        "gdn_a_log": jnp.log(jax.random.uniform(ks[8], (nB, GDN_HEADS), jnp.float32, 1.0, 16.0)),
        "gdn_dt_bias": dt + jnp.log(-jnp.expm1(-dt)),
        "gdn_norm_g": gain(ks[9], (nB, GDN_DV)),
        "gdn_w_out": nrm(ks[11], (nB, GDN_V, D_MODEL), GDN_V),
        "mla_w_in": nrm(ks[12], (nC, D_MODEL, MLA_IN), D_MODEL),
        "mla_q_norm_g": gain(ks[13], (nC, MLA_Q_LORA)),
        "mla_w_uq": nrm(ks[14], (nC, MLA_Q_LORA, MLA_HEADS * MLA_QK), MLA_Q_LORA),
        "mla_kv_norm_g": gain(ks[15], (nC, MLA_KV_LORA)),
        "mla_w_ukv": nrm(ks[16], (nC, MLA_KV_LORA, MLA_HEADS * (MLA_NOPE + MLA_V)), MLA_KV_LORA),
        "mla_w_out": nrm(ks[17], (nC, MLA_WIDTH, D_MODEL), MLA_WIDTH),
        "final_g": gain(ks[18], (D_MODEL,)),
    }


def reference(x, positions, norm_g, pool_w_in, pool_w_grp, pool_scale, pool_w_out,
              gdn_w_in, gdn_conv, gdn_a_log, gdn_dt_bias, gdn_norm_g, gdn_w_out,
              mla_w_in, mla_q_norm_g, mla_w_uq, mla_kv_norm_g, mla_w_ukv, mla_w_out,
              final_g):
    for i in range(DEPTH):
        kind, j = i % N_MIXERS, i // N_MIXERS
        h = rmsnorm(x, norm_g[i])
        if kind == 0:
            y = pool_mixer(h, pool_w_in[j], pool_w_grp[j], pool_scale[j], pool_w_out[j])
        elif kind == 1:
            y = gdn_mixer(h, gdn_w_in[j], gdn_conv[j], gdn_a_log[j], gdn_dt_bias[j],
                          gdn_norm_g[j], gdn_w_out[j])
        else:
            y = mla_mixer(h, positions, mla_w_in[j], mla_q_norm_g[j], mla_w_uq[j],
                          mla_kv_norm_g[j], mla_w_ukv[j], mla_w_out[j])
        x = x + y.astype(x.dtype)
    return rmsnorm(x, final_g)
```

```python
import contextlib
import numpy as np
import ml_dtypes
import concourse.bass as bass
import concourse.mybir as mybir
from concourse.bass_utils import run_bass_kernel_spmd

F32 = mybir.dt.float32
BF16 = mybir.dt.bfloat16
I32 = mybir.dt.int32
AF = mybir.ActivationFunctionType
ALU = mybir.AluOpType
AX = mybir.AxisListType

EPS = 1e-6
NCORES = 8


class Buf:
    __slots__ = ("name", "writer", "readers", "excl")

    def __init__(self, name):
        self.name = name
        self.excl = False
        self.writer = None
        self.readers = []


class _Eng:
    def __init__(self, name, samesync):
        self.name = name
        self.samesync = samesync
        self.sem = None
        self.count = 0
        self.ops = []
        self.seen = {}


class KB:
    EPOCH = 12000
    NDMA = 12

    def __init__(self, nc, stack):
        self.nc = nc
        self.stack = stack
        self.tstack = stack
        self.pfx = ""
        self.engs = {
            "pe": _Eng("pe", False),
            "act": _Eng("act", True),
            "dve": _Eng("dve", True),
            "pool": _Eng("pool", True),
            "sp": _Eng("sp", False),
        }
        self.nsem = 0
        for e in self.engs.values():
            e.sem = self._newsem(e.name)
        self.dma_sems = {}
        for q in ("sp", "pool", "act"):
            self.dma_sems[q] = [[self._newsem(f"d{q}{i}"), 0] for i in range(self.NDMA)]
        self.dma_rr = {"sp": 0, "pool": 0, "act": 0}
        self.out_events = []
        self.nbuf = 0

    def _newsem(self, name):
        self.nsem += 1
        return self.stack.enter_context(self.nc.semaphore(f"s_{name}_{self.nsem}"))

    def buf(self, name=None):
        self.nbuf += 1
        return Buf(name or f"b{self.nbuf}")

    def bufs(self, n, name="b"):
        return [self.buf(f"{name}{i}") for i in range(n)]

    def sbuf(self, name, shape, dtype):
        return self.tstack.enter_context(self.nc.sbuf_tensor("sb_" + self.pfx + name, list(shape), dtype))

    def psum(self, name, shape, dtype=F32):
        return self.tstack.enter_context(self.nc.psum_tensor("pp_" + self.pfx + name, list(shape), dtype))

    def _collect(self, eng, reads, writes):
        need = {}

        def add(ev):
            sem, val, en = ev
            if en == eng.name and not eng.samesync:
                return
            k = id(sem)
            if eng.seen.get(k, 0) >= val:
                return
            if k not in need or need[k][1] < val:
                need[k] = (sem, val)

        for b in reads:
            if b.writer is not None:
                add(b.writer)
            if b.excl:
                for r in b.readers:
                    if r[2] != eng.name:
                        add(r)
        for b in writes:
            if b.writer is not None:
                add(b.writer)
            for r in b.readers:
                add(r)
        waits = list(need.values())
        for sem, val in waits:
            eng.seen[id(sem)] = val
        return waits

    def _commit(self, ev, reads, writes):
        for b in reads:
            b.readers.append(ev)
            if len(b.readers) > 64:
                last = {}
                for r in b.readers:
                    k = id(r[0])
                    if k not in last or last[k][1] < r[1]:
                        last[k] = r
                b.readers = list(last.values())
        for b in writes:
            b.writer = ev
            b.readers = []

    def op(self, engname, fn, reads=(), writes=()):
        eng = self.engs[engname]
        waits = self._collect(eng, reads, writes)
        if eng.count >= self.EPOCH:
            eng.sem = self._newsem(eng.name)
            eng.count = 0
        eng.count += 1
        ev = (eng.sem, eng.count, eng.name)
        eng.ops.append((waits, fn, (eng.sem, 1)))
        self._commit(ev, reads, writes)
        return ev

    def dma(self, queue, out, in_, reads=(), writes=(), is_output=False, **kw):
        eng = self.engs[queue]
        slots = self.dma_sems[queue]
        i = self.dma_rr[queue]
        self.dma_rr[queue] = (i + 1) % len(slots)
        slot = slots[i]
        sem = slot[0]
        waits = self._collect(eng, reads, writes)
        if slot[1] > 0 and eng.seen.get(id(sem), 0) < slot[1]:
            waits.append((sem, slot[1]))
            eng.seen[id(sem)] = slot[1]
        slot[1] += 16
        ev = (sem, slot[1], "dma_" + queue)
        eng.ops.append((waits, (lambda e, o=out, a=in_, k=kw: e.dma_start(out=o, in_=a, **k)), (sem, 16)))
        self._commit(ev, reads, writes)
        if is_output:
            self.out_events.append(ev)
        return ev

    def allgather(self, out, in_, groups, reads=(), writes=()):
        eng = self.engs["pool"]
        sem = self._newsem("cc")
        waits = self._collect(eng, reads, writes)
        ev = (sem, 1, "cc")
        eng.ops.append((waits, (lambda e, o=out, a=in_, g=groups: e.collective_compute(
            "AllGather", ALU.bypass, replica_groups=g, ins=[a], outs=[o])), (sem, 1)))
        self._commit(ev, reads, writes)
        return ev

    def barrier(self):
        evs = []
        for e in self.engs.values():
            if e.count > 0:
                evs.append((e.sem, e.count, e.name))
        for q, slots in self.dma_sems.items():
            for sem, val in slots:
                if val > 0:
                    evs.append((sem, val, "dma_" + q))
        tok = Buf("barrier")
        for name, e in self.engs.items():
            waits = []
            for sem, val, en in evs:
                if en == name:
                    continue
                if e.seen.get(id(sem), 0) >= val:
                    continue
                waits.append((sem, val))
                e.seen[id(sem)] = val
            if waits:
                if e.count >= self.EPOCH:
                    e.sem = self._newsem(e.name)
                    e.count = 0
                e.count += 1
                e.ops.append((waits, (lambda en_: en_.engine_nop()) if name in ("dve", "pool") else None, (e.sem, 1)))

    def flush(self, final=False):
        final_waits = []
        if final:
            last = {}
            for sem, val, _ in self.out_events:
                k = id(sem)
                if k not in last or last[k][1] < val:
                    last[k] = (sem, val)
            final_waits = list(last.values())
        nc = self.nc
        engs = self.engs
        self.nflush = getattr(self, "nflush", 0) + 1
        with nc.named_scope(f"{self.pfx}f{self.nflush}"), nc.Block() as block:
            def replay(e, st, extra=None):
                for waits, fn, inc in st.ops:
                    for sem, val in waits:
                        e.wait_ge(sem, val)
                    if fn is None:
                        e.sem_inc(inc[0], inc[1])
                        continue
                    ins = fn(e)
                    ins.then_inc(inc[0], inc[1])
                if extra:
                    for sem, val in extra:
                        e.wait_ge(sem, val)

            @block.tensor
            def _(e):
                replay(e, engs["pe"])

            @block.scalar
            def _(e):
                replay(e, engs["act"])

            @block.vector
            def _(e):
                replay(e, engs["dve"])

            @block.gpsimd
            def _(e):
                replay(e, engs["pool"])

            @block.sync
            def _(e):
                replay(e, engs["sp"], final_waits)
        for e in engs.values():
            e.ops = []

    def finish(self):
        self.flush(final=True)


def MM(kb, out, lhsT, rhs, start, stop, reads, writes):
    return kb.op("pe", lambda e: e.matmul(out, lhsT=lhsT, rhs=rhs, start=start, stop=stop), reads, writes)


def TR(kb, out, in_, ident, reads, writes):
    return kb.op("pe", lambda e: e.transpose(out, in_, ident), reads, writes)


def ACT(kb, out, in_, func, reads, writes, **kw):
    return kb.op("act", lambda e: e.activation(out, in_, func, **kw), reads, writes)


def TT(kb, eng, out, in0, in1, op, reads, writes):
    return kb.op(eng, lambda e: e.tensor_tensor(out, in0, in1, op), reads, writes)


def TS(kb, eng, out, in0, s1, s2, op0, op1, reads, writes):
    if op1 is None:
        return kb.op(eng, lambda e: e.tensor_scalar(out, in0, s1, None, op0=op0), reads, writes)
    return kb.op(eng, lambda e: e.tensor_scalar(out, in0, s1, s2, op0=op0, op1=op1), reads, writes)


def STT(kb, eng, out, in0, scalar, in1, op0, op1, reads, writes):
    return kb.op(eng, lambda e: e.scalar_tensor_tensor(out, in0, scalar, in1, op0=op0, op1=op1), reads, writes)


def CP(kb, eng, out, in_, reads, writes):
    if eng == "act":
        return kb.op("act", lambda e: e.copy(out, in_), reads, writes)
    return kb.op(eng, lambda e: e.tensor_copy(out, in_), reads, writes)


def RECIP(kb, out, in_, reads, writes):
    return kb.op("dve", lambda e: e.reciprocal(out, in_), reads, writes)


def MEMSET(kb, eng, ap, val, writes):
    return kb.op(eng, lambda e: e.memset(ap, val), (), writes)


class PsumRing:
    def __init__(self, kb, n=8, name="ps"):
        self.t = [kb.psum(f"{name}{i}", [128, 512], F32) for i in range(n)]
        self.b = kb.bufs(n, name)
        for b in self.b:
            b.excl = True
        self.i = 0
        self.n = n
        self.held = [False] * n

    def _pick(self):
        for _ in range(self.n):
            i = self.i
            self.i = (i + 1) % self.n
            if not self.held[i]:
                return i
        raise RuntimeError("all PSUM banks held")

    def next(self):
        i = self._pick()
        return self.t[i], self.b[i]

    def acquire(self):
        i = self._pick()
        self.held[i] = True
        return self.t[i], self.b[i], i

    def release(self, i):
        self.held[i] = False


def rms_stats(kb, ring, ones, b_ones, sq_t, b_sq, srcs, width, rstd_ap, b_rstd, inv_n, tmp_ap=None):
    ps, bps = ring.next()
    n = len(srcs)
    for i, (ap, b) in enumerate(srcs):
        j = i % len(sq_t)
        ACT(kb, sq_t[j][:, 0:width], ap, AF.Square, [b], [b_sq[j]])
        MM(kb, ps[:, 0:width], ones, sq_t[j][:, 0:width], i == 0, i == n - 1, [b_ones, b_sq[j]], [bps])
    TS(kb, "dve", rstd_ap, ps[:, 0:width], inv_n, EPS, ALU.mult, ALU.add, [bps], [b_rstd])
    RECIP(kb, rstd_ap, rstd_ap, [b_rstd], [b_rstd])
    ACT(kb, rstd_ap, rstd_ap, AF.Sqrt, [b_rstd], [b_rstd])


NT = 2048
HAL = 16
TTOK = 512
POOL_WINDOWS = (2, 4, 8, 16)


NT = 2048
HAL = 16
TTOK = 512
POOL_WINDOWS = (2, 4, 8, 16)


class NS:
    pass


def emit_tok(nc, kb, D, do_outproj, do_pool, do_final):
    W = NT + HAL
    with contextlib.ExitStack() as st_:
        kb.tstack = st_
        kb.pfx = D.pfx
        ring = PsumRing(kb, 8)
        xres = kb.sbuf("xres", [128, 8, W], F32)
        bx = [[kb.buf(f"x{kc}_{t}") for t in range(5)] for kc in range(8)]
        ones = kb.sbuf("ones", [128, 128], BF16); b_ones = kb.buf("ones")
        MEMSET(kb, "pool", ones[:], 1.0, [b_ones])
        sq_t = [kb.sbuf(f"sq{i}", [128, TTOK], BF16) for i in range(2)]; b_sq = kb.bufs(2, "sq")
        rstd = kb.sbuf("rstd", [128, TTOK], F32); b_rstd = kb.buf("rstd")
        Wo = [kb.sbuf(f"Wo{i}", [128, 4, 1024], BF16) for i in range(2)]; b_Wo = kb.bufs(2, "Wo")
        ybuf = [kb.sbuf(f"ybuf{i}", [128, 4, TTOK], BF16) for i in range(2)]; b_y = kb.bufs(2, "ybuf")
        flg = kb.sbuf("flg", [128, 2], F32); b_flg = kb.buf("flg")
        kb.dma("sp", flg[:], D.flag_d, writes=[b_flg])

        def colrange(t):
            return (0, HAL) if t == 0 else (HAL + (t - 1) * TTOK, HAL + t * TTOK)

        for t in range(1, 5):
            xm = D.x_main[t - 1].rearrange("(kc p) t -> p kc t", p=128)
            for kc in range(8):
                kb.dma("sp", xres[:, kc, HAL + (t - 1) * TTOK:HAL + t * TTOK], xm[:, kc, :], reads=D.x_reads, writes=[bx[kc][t]])
        if D.x_halo is not None:
            xh = D.x_halo.rearrange("(kc p) t -> p kc t", p=128)
            for kc in range(8):
                kb.dma("sp", xres[:, kc, 0:HAL], xh[:, kc, :], reads=D.xh_reads, writes=[bx[kc][0]])
                if D.halo_flag:
                    TS(kb, "dve", xres[:, kc, 0:HAL], xres[:, kc, 0:HAL], flg[:, 0:1], None, ALU.mult, None,
                       [bx[kc][0], b_flg], [bx[kc][0]])
        else:
            for kc in range(8):
                MEMSET(kb, "pool", xres[:, kc, 0:HAL], 0.0, [bx[kc][0]])

        def outproj_stage(Wo_t, b_w, y_t, b_yy, t, nkc=4):
            c0, c1 = colrange(t)
            for oc in range(8):
                ps, bps = ring.next()
                for kc in range(nkc):
                    MM(kb, ps[:, 0:c1 - c0], Wo_t[:, kc, oc * 128:(oc + 1) * 128], y_t[:, kc, 0:c1 - c0],
                       kc == 0, kc == nkc - 1, [b_w, b_yy], [bps])
                TT(kb, "dve", xres[:, oc, c0:c1], xres[:, oc, c0:c1], ps[:, 0:c1 - c0], ALU.add,
                   [bps, bx[oc][t]], [bx[oc][t]])

        ycnt = 0
        if do_outproj:
            with contextlib.ExitStack() as st1:
                kb.tstack = st1
                ya = [kb.sbuf(f"ya{i}", [128, 4, TTOK], BF16) for i in range(2)]; b_ya = kb.bufs(2, "ya")
                yb_ = [kb.sbuf(f"yb{i}", [128, 4, TTOK], BF16) for i in range(2)]; b_yb = kb.bufs(2, "yb")
                wpv = D.wprev_d.rearrange("(g kc p) n -> g p kc n", p=128, kc=4)
                y0v = [a.rearrange("(g kc p) t -> g p kc t", p=128, kc=4) for a in D.y0]
                y1v = [a.rearrange("(g kc p) t -> g p kc t", p=128, kc=4) for a in D.y1]
                yhv = D.y_halo.rearrange("(g kc p) t -> g p kc t", p=128, kc=4) if D.y_halo is not None else None
                for gi in range(4):
                    kb.dma("pool", Wo[gi % 2][:], wpv[gi], writes=[b_Wo[gi % 2]])
                    for t in range(0 if do_pool else 1, 5):
                        yb = ycnt % 2
                        ycnt += 1
                        if t == 0:
                            kb.dma("sp", ya[yb][:, :, 0:HAL], yhv[gi], reads=D.y_reads, writes=[b_ya[yb]])
                            TS(kb, "dve", ybuf[yb][:, :, 0:HAL], ya[yb][:, :, 0:HAL], flg[:, 0:1], None, ALU.mult, None,
                               [b_ya[yb], b_flg], [b_y[yb]])
                        else:
                            kb.dma("sp", ya[yb][:], y0v[t - 1][gi], reads=D.y_reads, writes=[b_ya[yb]])
                            kb.dma("sp", yb_[yb][:], y1v[t - 1][gi], reads=D.y_reads, writes=[b_yb[yb]])
                            TS(kb, "dve", ya[yb][:], ya[yb][:], flg[:, 1:2], None, ALU.mult, None, [b_ya[yb], b_flg], [b_ya[yb]])
                            STT(kb, "dve", ybuf[yb][:], yb_[yb][:], flg[:, 0:1], ya[yb][:], ALU.mult, ALU.add,
                                [b_yb[yb], b_ya[yb], b_flg], [b_y[yb]])
                        outproj_stage(Wo[gi % 2], b_Wo[gi % 2], ybuf[yb], b_y[yb], t)
                kb.barrier()
                kb.flush()
            kb.tstack = st_

        with contextlib.ExitStack() as st2:
            kb.tstack = st2
            if do_pool:
                g_d, win_d, wgrp_d, wout_d, scale_d, invc_d = D.g_d, D.win_d, D.wgrp_d, D.wout_d, D.scale_d, D.invc_d
                gsb = kb.sbuf("gsb", [128, 8], F32); b_g = kb.buf("g")
                scl = kb.sbuf("scl", [128, 16], F32); b_scl = kb.buf("scl")
                invc = kb.sbuf("invc", [128, 64], F32); b_invc = kb.buf("invc")
                kb.dma("sp", gsb[:], g_d, writes=[b_g])
                kb.dma("sp", scl[:], scale_d, writes=[b_scl])
                kb.dma("sp", invc[:], invc_d, writes=[b_invc])
                hT = kb.sbuf("hT", [128, 8, W], BF16)
                bh = [kb.buf(f"h{t}") for t in range(5)]
                W1 = [kb.sbuf(f"W1_{i}", [128, 8, 1024], BF16) for i in range(2)]; b_W1 = kb.bufs(2, "W1")
                Wg = [kb.sbuf(f"Wg{i}", [128, 4, 512], BF16) for i in range(2)]; b_Wg = kb.bufs(2, "Wg")
                ubuf = [kb.sbuf(f"ubuf{i}", [128, 4, TTOK + HAL], F32) for i in range(2)]
                b_u = [[kb.buf(f"u{i}_{cc}") for cc in range(4)] for i in range(2)]
                b_uc = [[kb.buf(f"uc{i}_{cc}") for cc in range(4)] for i in range(2)]
                sg = [kb.sbuf(f"sg{i}", [128, 4, TTOK], BF16) for i in range(2)]; b_sg = [kb.bufs(4, f"sg{i}_") for i in range(2)]
                tmp = [kb.sbuf(f"ptmp{i}", [128, TTOK + HAL], F32) for i in range(2)]; b_tmp = kb.bufs(2, "ptmp")
                t16 = kb.sbuf("t16", [128, HAL], F32); b_t16 = kb.buf("t16")
                pbuf = [kb.sbuf(f"pbuf{i}", [128, 4, TTOK], BF16) for i in range(2)]; b_p = kb.bufs(2, "pbuf")

                for t in range(5):
                    c0, c1 = colrange(t)
                    wd = c1 - c0
                    rms_stats(kb, ring, ones[:], b_ones, sq_t, b_sq,
                              [(xres[:, kc, c0:c1], bx[kc][t]) for kc in range(8)], wd, rstd[:, 0:wd], b_rstd, 1.0 / 1024)
                    for kc in range(8):
                        STT(kb, "dve", hT[:, kc, c0:c1], xres[:, kc, c0:c1], gsb[:, kc:kc + 1], rstd[:, 0:wd],
                            ALU.mult, ALU.mult, [bx[kc][t], b_g, b_rstd], [bh[t]])

                winv = win_d.rearrange("(kc p) n -> p kc n", p=128)
                wgv = wgrp_d.rearrange("g (kc p) n -> g p kc n", p=128)
                wov = wout_d.rearrange("(g kc p) n -> g p kc n", p=128, kc=4)

                def load_group(gi):
                    s = gi % 2
                    for kc in range(8):
                        kb.dma("pool", W1[s][:, kc, 0:512], winv[:, kc, gi * 512:(gi + 1) * 512], writes=[b_W1[s]])
                        kb.dma("pool", W1[s][:, kc, 512:1024], winv[:, kc, 2048 + gi * 512:2048 + (gi + 1) * 512],
                               writes=[b_W1[s]])
                    kb.dma("pool", Wg[s][:], wgv[gi], writes=[b_Wg[s]])
                    kb.dma("pool", Wo[s][:], wov[gi], writes=[b_Wo[s]])

                load_group(0)
                load_group(1)

                def ub_of(gi, t):
                    return (gi * 4 + t - 1) % 2

                def emit_halo(gi):
                    s = gi % 2
                    ub = ub_of(gi, 1)
                    for cc in range(4):
                        ps, bps = ring.next()
                        for kc in range(8):
                            MM(kb, ps[:, 0:HAL], W1[s][:, kc, cc * 128:(cc + 1) * 128], hT[:, kc, 0:HAL],
                               kc == 0, kc == 7, [b_W1[s], bh[0]], [bps])
                        CP(kb, "act", ubuf[ub][:, cc, 0:HAL], ps[:, 0:HAL], [bps], [b_uc[ub][cc]])

                def emit_inproj(gi, t):
                    s = gi % 2
                    c0, c1 = colrange(t)
                    ub = ub_of(gi, t)
                    sgi = t % 2
                    for cc in range(4):
                        ps, bps = ring.next()
                        for kc in range(8):
                            MM(kb, ps[:], W1[s][:, kc, cc * 128:(cc + 1) * 128], hT[:, kc, c0:c1],
                               kc == 0, kc == 7, [b_W1[s], bh[t]], [bps])
                        CP(kb, "act", ubuf[ub][:, cc, HAL:HAL + TTOK], ps[:], [bps], [b_u[ub][cc]])
                    for cc in range(4):
                        ps, bps = ring.next()
                        for kc in range(8):
                            MM(kb, ps[:], W1[s][:, kc, 512 + cc * 128:512 + (cc + 1) * 128], hT[:, kc, c0:c1],
                               kc == 0, kc == 7, [b_W1[s], bh[t]], [bps])
                        ACT(kb, sg[sgi][:, cc, :], ps[:], AF.Silu, [bps], [b_sg[sgi][cc]])

                def emit_rest(gi, t):
                    nonlocal ycnt
                    s = gi % 2
                    win = POOL_WINDOWS[gi]
                    ub = ub_of(gi, t)
                    nub = 1 - ub
                    sgi = t % 2
                    pb = t % 2
                    for cc in range(4):
                        U = ubuf[ub][:, cc, :]
                        ru = [b_u[ub][cc], b_uc[ub][cc]]
                        ta, tb = tmp[0], tmp[1]
                        bta, btb = b_tmp[0], b_tmp[1]
                        Wd = TTOK + HAL
                        TT(kb, "dve", ta[:, 1:Wd], U[:, 1:Wd], U[:, 0:Wd - 1], ALU.add, ru, [bta])
                        cur, bcur, oth, both = ta, bta, tb, btb
                        sh = 2
                        lo = 1
                        while sh < win:
                            lo2 = lo + sh
                            TT(kb, "dve", oth[:, lo2:Wd], cur[:, lo2:Wd], cur[:, lo2 - sh:Wd - sh], ALU.add, [bcur], [both])
                            cur, bcur, oth, both = oth, both, cur, bcur
                            lo = lo2
                            sh *= 2
                        STT(kb, "dve", pbuf[pb][:, cc, :], cur[:, HAL:Wd], 1.0 / win, U[:, HAL:Wd],
                            ALU.mult, ALU.subtract, [bcur] + ru, [b_p[pb]])
                        if t == 1:
                            TT(kb, "dve", t16[:], cur[:, HAL:2 * HAL], invc[:, gi * 16:(gi + 1) * 16], ALU.mult,
                               [bcur, b_invc], [b_t16])
                            TT(kb, "dve", pbuf[pb][:, cc, 0:HAL], t16[:], U[:, HAL:2 * HAL], ALU.subtract,
                               [b_t16] + ru, [b_p[pb]])
                        if t < 4:
                            CP(kb, "pool", ubuf[nub][:, cc, 0:HAL], U[:, TTOK:TTOK + HAL], ru, [b_uc[nub][cc]])
                    yb = ycnt % 2
                    ycnt += 1
                    for oc in range(4):
                        ps, bps = ring.next()
                        for kc in range(4):
                            MM(kb, ps[:], Wg[s][:, kc, oc * 128:(oc + 1) * 128], pbuf[pb][:, kc, :],
                               kc == 0, kc == 3, [b_Wg[s], b_p[pb]], [bps])
                        STT(kb, "dve", ybuf[yb][:, oc, :], ps[:], scl[:, gi * 4 + oc:gi * 4 + oc + 1], sg[sgi][:, oc, :],
                            ALU.mult, ALU.mult, [bps, b_scl, b_sg[sgi][oc]], [b_y[yb]])
                    outproj_stage(Wo[s], b_Wo[s], ybuf[yb], b_y[yb], t)

                seq_ = [(gi, t) for gi in range(4) for t in range(1, 5)]
                emit_halo(0)
                emit_inproj(0, 1)
                for i_, (gi, t) in enumerate(seq_):
                    if i_ + 1 < len(seq_):
                        ngi, nt = seq_[i_ + 1]
                        if nt == 1:
                            emit_halo(ngi)
                        emit_inproj(ngi, nt)
                    emit_rest(gi, t)
                    if t == 4 and gi + 2 < 4:
                        load_group(gi + 2)
            if do_final:
                fgs = kb.sbuf("fgs", [128, 8], F32); b_fg = kb.buf("fg")
                kb.dma("sp", fgs[:], D.fg_d, writes=[b_fg])
                for t in range(1, 5):
                    c0, c1 = colrange(t)
                    rms_stats(kb, ring, ones[:], b_ones, sq_t, b_sq,
                              [(xres[:, kc, c0:c1], bx[kc][t]) for kc in range(8)], TTOK, rstd[:], b_rstd, 1.0 / 1024)
                    for kc in range(8):
                        STT(kb, "dve", xres[:, kc, c0:c1], xres[:, kc, c0:c1], fgs[:, kc:kc + 1], rstd[:],
                            ALU.mult, ALU.mult, [bx[kc][t], b_fg, b_rstd], [bx[kc][t]])
            for t in range(1, 5):
                c0, c1 = colrange(t)
                ov = D.out[t - 1].rearrange("(kc p) t -> p kc t", p=128)
                for kc in range(8):
                    kb.dma("sp", ov[:, kc, :], xres[:, kc, c0:c1], reads=[bx[kc][t]],
                           writes=([] if D.out_is_output else [D.b_out[t - 1]]), is_output=D.out_is_output)
                if D.after_tile is not None:
                    D.after_tile(t - 1)
            kb.barrier()
            kb.flush()
    kb.tstack = kb.stack


def _colmajor128(v, n):
    return np.ascontiguousarray(np.asarray(v, np.float32).reshape(n, 128).T)


def _halo_T(x_b, r):
    F = x_b.shape[1]
    out = np.zeros((F, HAL + NT), x_b.dtype)
    lo = r * NT
    if r > 0:
        out[:, :HAL] = x_b[lo - HAL:lo].T
    out[:, HAL:] = x_b[lo:lo + NT].T
    return out


def _invcnt(r):
    t = np.arange(16)
    tab = np.zeros((128, 64), np.float32)
    for gi, w in enumerate(POOL_WINDOWS):
        cnt = np.minimum(t + 1, w) if r == 0 else np.full(16, w)
        tab[:, gi * 16:(gi + 1) * 16] = (1.0 / cnt.astype(np.float32))[None, :]
    return tab


_NC_CACHE = {}


def _get_nc(key, builder):
    if key not in _NC_CACHE:
        _NC_CACHE[key] = builder()
    return _NC_CACHE[key]


SEQ = 4096
GH = 4
NEG = -30000.0


SEQ = 4096
GH = 4
MH = 8
NEG = -30000.0
C1_2PI = 6.28125
C2_2PI = 2.0 * np.pi - 6.28125


def emit_gdn(nc, kb, D, ntile=8, dbg=9):
    S = ntile * TTOK
    with contextlib.ExitStack() as st_:
        kb.tstack = st_
        kb.pfx = D.pfx
        ring = PsumRing(kb, 8)
        ones = kb.sbuf("ones", [128, 128], F32); b_ones = kb.buf("ones")
        MEMSET(kb, "pool", ones[:], 1.0, [b_ones])
        onesb = kb.sbuf("onesb", [128, 128], BF16); b_onesb = kb.buf("onesb")
        MEMSET(kb, "pool", onesb[:], 1.0, [b_onesb])
        cst = kb.sbuf("cst", [128, 5 * 128], F32); b_cst = kb.buf("cst")
        kb.dma("sp", cst[:], D.cst_d, writes=[b_cst])
        ident, TRI, MASKI, MASKS, MASKP = [cst[:, i * 128:(i + 1) * 128] for i in range(5)]
        gsb = kb.sbuf("gsb", [128, 8], F32); b_g = kb.buf("g")
        kb.dma("sp", gsb[:], D.g_d, writes=[b_g])
        cw = kb.sbuf("cw", [128, 64], F32); b_cw = kb.buf("cw")
        kb.dma("sp", cw[:], D.cw_d, writes=[b_cw])
        hp = kb.sbuf("hp", [128, 8], F32); b_hp = kb.buf("hp")
        kb.dma("sp", hp[:], D.hp_d, writes=[b_hp])
        ngb = kb.sbuf("ngb", [128, 256], F32); b_ng = kb.buf("ng")
        kb.dma("sp", ngb[:], D.ng_d, writes=[b_ng])
        nA = kb.sbuf("nA", [128, 4], F32); b_nA = kb.buf("nA")
        ACT(kb, nA[:], hp[:, 0:4], AF.Exp, [b_hp], [b_nA])
        TS(kb, "dve", nA[:], nA[:], -1.0, None, ALU.mult, None, [b_nA], [b_nA])

        Wf = kb.sbuf("Wf", [128, 8, GH * 512], BF16); b_Wf = kb.bufs(8, "Wf")
        Wg = kb.sbuf("Wg", [128, 8, GH * 256], BF16); b_Wg = kb.bufs(8, "Wg")
        Wba = kb.sbuf("Wba", [128, 8, 8], BF16); b_Wba = kb.buf("Wba")
        wfv = D.wf_d.rearrange("(kc p) n -> p kc n", p=128)
        wgv = D.wg_d.rearrange("(kc p) n -> p kc n", p=128)
        for kc in range(8):
            kb.dma("pool", Wf[:, kc, :], wfv[:, kc, :], writes=[b_Wf[kc]])
        for kc in range(8):
            kb.dma("pool", Wg[:, kc, :], wgv[:, kc, :], writes=[b_Wg[kc]])
        kb.dma("pool", Wba[:], D.wba_d.rearrange("(kc p) n -> p kc n", p=128), writes=[b_Wba])

        xt = kb.sbuf("xt", [128, 8, TTOK], F32); b_xt = kb.bufs(8, "xt")
        hT = kb.sbuf("hT", [128, 8, TTOK], BF16); b_h = kb.buf("hT")
        sq_t = [kb.sbuf(f"sq{i}", [128, TTOK], BF16) for i in range(2)]; b_sq = kb.bufs(2, "sq")
        rstd = kb.sbuf("rstd", [128, TTOK], F32); b_rstd = kb.buf("rstd")
        raw = [kb.sbuf(f"raw{i}", [128, TTOK + 3], F32) for i in range(4)]; b_raw = kb.bufs(4, "raw")
        carry = kb.sbuf("carry", [128, 16, 3], F32); b_carry = kb.bufs(16, "carry")
        MEMSET(kb, "pool", carry[:], 0.0, b_carry)
        cv = [kb.sbuf(f"cv{i}", [128, TTOK], F32) for i in range(2)]; b_cv = kb.bufs(2, "cv")
        qk32 = [kb.sbuf(f"qk32_{i}", [128, TTOK], F32) for i in range(2)]; b_qk32 = kb.bufs(2, "qk32")
        kn32 = kb.sbuf("kn32", [128, TTOK], F32); b_kn32 = kb.buf("kn32")
        rs = kb.sbuf("rs", [128, TTOK], F32); b_rs = kb.buf("rs")
        KQ = [kb.sbuf(f"KQ{h}", [128, 4, 2, 128], BF16) for h in range(GH)]
        b_KQ = [kb.bufs(2, f"KQ{h}_") for h in range(GH)]
        sgate = [kb.sbuf(f"sgate{c}", [128, GH * 256], BF16) for c in range(4)]; b_sgate = kb.bufs(4, "sgate")
        bsig = [kb.sbuf(f"bsig{c}", [128, 4], F32) for c in range(4)]
        lnb = [kb.sbuf(f"lnb{c}", [128, 4], F32) for c in range(4)]
        gcol = [kb.sbuf(f"gcol{c}", [128, 4], F32) for c in range(4)]
        gc = [kb.sbuf(f"gc{c}", [128, 4], F32) for c in range(4)]
        ngc = [kb.sbuf(f"ngc{c}", [128, 4], F32) for c in range(4)]
        gl = [kb.sbuf(f"gl{c}", [128, 4], F32) for c in range(4)]
        beg = [kb.sbuf(f"beg{c}", [128, 4], F32) for c in range(4)]
        kds = [kb.sbuf(f"kds{c}", [128, 4], F32) for c in range(4)]
        glast = [kb.sbuf(f"glast{c}", [128, 4], F32) for c in range(4)]
        b_tok = kb.bufs(4, "tokscal")
        tmp8 = kb.sbuf("tmp8", [128, 8], F32); b_tmp8 = kb.buf("tmp8")
        spt = kb.sbuf("spt", [128, 16], F32)
        Kbg = [[kb.sbuf(f"Kbg{h}_{c}", [128, 128], BF16) for c in range(4)] for h in range(GH)]
        Kdec = [[kb.sbuf(f"Kdec{h}_{c}", [128, 128], BF16) for c in range(4)] for h in range(GH)]
        Vb = [[kb.sbuf(f"Vb{h}_{c}", [128, 256], BF16) for c in range(4)] for h in range(GH)]
        b_tm = [[kb.buf(f"tm{h}_{c}") for c in range(4)] for h in range(GH)]
        NSET = 4
        Gb = [kb.sbuf(f"Gb{i}", [128, 128], F32) for i in range(NSET)]; b_Gb = kb.bufs(NSET, "Gb")
        LNBb = [kb.sbuf(f"LNBb{i}", [128, 128], F32) for i in range(NSET)]; b_LNBb = kb.bufs(NSET, "LNBb")
        EG = [kb.sbuf(f"EG{i}", [128, 128], F32) for i in range(NSET)]; b_EG = kb.bufs(NSET, "EG")
        EGl = kb.sbuf("EGl", [128, 16], F32)
        DT = [kb.sbuf(f"DT{i}", [128, 3, 128], F32) for i in range(NSET)]; b_DT = kb.bufs(NSET, "DT")
        XA = [[kb.sbuf(f"XA{i}_{j}", [128, 128], F32) for j in range(2)] for i in range(NSET)]
        XTA = [[kb.sbuf(f"XTA{i}_{j}", [128, 128], F32) for j in range(2)] for i in range(NSET)]
        RA = [[kb.sbuf(f"RA{i}_{j}", [128, 128], F32) for j in range(2)] for i in range(NSET)]
        b_XA = [kb.bufs(2, f"XA{i}_") for i in range(NSET)]
        b_XTA = [kb.bufs(2, f"XTA{i}_") for i in range(NSET)]
        b_RA = [kb.bufs(2, f"RA{i}_") for i in range(NSET)]
        Rb = [kb.sbuf(f"Rb{i}", [128, 128], BF16) for i in range(NSET)]; b_Rb = kb.bufs(NSET, "Rb")
        AT = [[kb.sbuf(f"AT{h}_{c}", [128, 128], BF16) for c in range(2)] for h in range(GH)]
        U = [[kb.sbuf(f"U{h}_{c}", [128, 256], F32) for c in range(2)] for h in range(GH)]
        nWT = [[kb.sbuf(f"nWT{h}_{c}", [128, 128], F32) for c in range(2)] for h in range(GH)]
        QgT = [[kb.sbuf(f"QgT{h}_{c}", [128, 128], F32) for c in range(2)] for h in range(GH)]
        b_pre = [[kb.buf(f"pre{h}_{c}") for c in range(2)] for h in range(GH)]
        Vn = [kb.sbuf(f"Vn{h}", [128, 256], BF16) for h in range(GH)]; b_Vn = kb.bufs(GH, "Vn")
        Sst = [kb.sbuf(f"S{h}", [128, 256], F32) for h in range(GH)]; b_S = kb.bufs(GH, "S")
        for h in range(GH):
            MEMSET(kb, "pool", Sst[h][:], 0.0, [b_S[h]])
        osq = [kb.sbuf(f"osq{h}", [128, 256], F32) for h in range(GH)]; b_osq = kb.bufs(GH, "osq")
        oss = [kb.sbuf(f"oss{h}", [128, 1], F32) for h in range(GH)]; b_oss = kb.bufs(GH, "oss")
        o32 = [kb.sbuf(f"o32_{h}", [128, 256], F32) for h in range(GH)]; b_o32 = kb.bufs(GH, "o32")
        ogt = [kb.sbuf(f"ogt{i}", [128, GH * 256], F32) for i in range(2)]; b_ogt = kb.bufs(2, "ogt")
        ogT = [kb.sbuf(f"ogT{i}", [128, 8, 128], BF16) for i in range(2)]; b_ogT = kb.bufs(2, "ogT")

        ogcnt = 0
        for T in range(ntile):
            t0 = T * TTOK
            for kc in range(8):
                kb.dma("sp", xt[:, kc, :], D.xtile(T, kc), reads=D.x_reads, writes=[b_xt[kc]])
            rms_stats(kb, ring, onesb[:], b_onesb, sq_t, b_sq, [(xt[:, kc, :], b_xt[kc]) for kc in range(8)],
                      TTOK, rstd[:], b_rstd, 1.0 / 1024)
            for kc in range(8):
                STT(kb, "dve", hT[:, kc, :], xt[:, kc, :], gsb[:, kc:kc + 1], rstd[:], ALU.mult, ALU.mult,
                    [b_xt[kc], b_g, b_rstd], [b_h])
            for c in range(4 if dbg >= 2 else 0):
                cs = slice(c * 128, (c + 1) * 128)
                for half in range(2):
                    ps, bps = ring.next()
                    for kc in range(8):
                        MM(kb, ps[:], hT[:, kc, cs], Wg[:, kc, half * 512:(half + 1) * 512], kc == 0, kc == 7,
                           [b_h, b_Wg[kc]], [bps])
                    ACT(kb, sgate[c][:, half * 512:(half + 1) * 512], ps[:], AF.Silu, [bps], [b_sgate[c]])
                ps, bps = ring.next()
                for kc in range(8):
                    MM(kb, ps[:, 0:8], hT[:, kc, cs], Wba[:, kc, :], kc == 0, kc == 7, [b_h, b_Wba], [bps])
                bt = b_tok[c]
                ACT(kb, bsig[c][:], ps[:, 0:4], AF.Sigmoid, [bps], [bt])
                ACT(kb, lnb[c][:], bsig[c][:], AF.Ln, [bt], [bt])
                TT(kb, "dve", tmp8[:, 0:4], ps[:, 4:8], hp[:, 4:8], ALU.add, [bps, b_hp], [b_tmp8])
                z_ = tmp8[:, 0:4]; y_ = tmp8[:, 4:8]
                w_ = spt[:, 0:4]; w2_ = spt[:, 4:8]; P_ = spt[:, 8:12]; m_ = spt[:, 12:16]
                t8 = [b_tmp8]
                TS(kb, "dve", y_, z_, -1.0, None, ALU.mult, None, t8, t8)
                TT(kb, "dve", y_, y_, z_, ALU.max, t8, t8)
                ACT(kb, y_, y_, AF.Exp, t8, t8, scale=-1.0)
                TS(kb, "dve", w_, y_, 2.0, None, ALU.add, None, t8, t8)
                RECIP(kb, w_, w_, t8, t8)
                TT(kb, "dve", w_, w_, y_, ALU.mult, t8, t8)
                TT(kb, "dve", w2_, w_, w_, ALU.mult, t8, t8)
                TS(kb, "dve", P_, w2_, 1.0 / 15, 1.0 / 13, ALU.mult, ALU.add, t8, t8)
                for cf in (11, 9, 7, 5, 3, 1):
                    TT(kb, "dve", P_, P_, w2_, ALU.mult, t8, t8)
                    TS(kb, "dve", P_, P_, 1.0 / cf, None, ALU.add, None, t8, t8)
                TT(kb, "dve", P_, P_, w_, ALU.mult, t8, t8)
                TS(kb, "dve", m_, z_, 0.0, None, ALU.max, None, t8, t8)
                STT(kb, "dve", P_, P_, 2.0, m_, ALU.mult, ALU.add, t8, t8)
                TT(kb, "dve", gcol[c][:], P_, nA[:], ALU.mult, [b_tmp8, b_nA], [bt])
                ps2, bps2 = ring.next()
                MM(kb, ps2[:, 0:4], TRI, gcol[c][:], True, True, [b_cst, bt], [bps2])
                CP(kb, "dve", gc[c][:], ps2[:, 0:4], [bps2], [bt])
                TS(kb, "dve", ngc[c][:], ps2[:, 0:4], -1.0, None, ALU.mult, None, [bps2], [bt])
                TT(kb, "dve", gl[c][:], gc[c][:], lnb[c][:], ALU.add, [bt], [bt])
                ACT(kb, beg[c][:], gl[c][:], AF.Exp, [bt], [bt])
                ps3, bps3 = ring.next()
                MM(kb, ps3[:, 0:4], ones[:], gcol[c][:], True, True, [b_ones, bt], [bps3])
                CP(kb, "dve", glast[c][:], ps3[:, 0:4], [bps3], [bt])
                TT(kb, "dve", kds[c][:], glast[c][:], gc[c][:], ALU.subtract, [bt], [bt])
                ACT(kb, kds[c][:], kds[c][:], AF.Exp, [bt], [bt])
            for h in range(GH if dbg >= 1 else 0):
                for p in range(4):
                    ci = h * 4 + p
                    col = h * 512 + p * 128
                    ps, bps = ring.next()
                    for kc in range(8):
                        MM(kb, ps[:], Wf[:, kc, col:col + 128], hT[:, kc, :], kc == 0, kc == 7, [b_Wf[kc], b_h], [bps])
                    CP(kb, "pool", raw[p][:, 0:3], carry[:, ci, :], [b_carry[ci]], [b_raw[p]])
                    CP(kb, "act", raw[p][:, 3:3 + TTOK], ps[:], [bps], [b_raw[p]])
                    CP(kb, "pool", carry[:, ci, :], raw[p][:, TTOK:TTOK + 3], [b_raw[p]], [b_carry[ci]])
                    ce = "dve"
                    c_ = cv[p % 2]; bc_ = b_cv[p % 2]
                    TS(kb, ce, c_[:], raw[p][:, 3:3 + TTOK], cw[:, ci * 4 + 3:ci * 4 + 4], None, ALU.mult, None,
                       [b_raw[p], b_cw], [bc_])
                    for k in (2, 1, 0):
                        STT(kb, ce, c_[:], raw[p][:, k:k + TTOK], cw[:, ci * 4 + k:ci * 4 + k + 1], c_[:],
                            ALU.mult, ALU.add, [b_raw[p], b_cw, bc_], [bc_])
                    if p < 2:
                        ACT(kb, qk32[p][:], c_[:], AF.Silu, [bc_], [b_qk32[p]])
                        rms_stats(kb, ring, onesb[:], b_onesb, sq_t, b_sq, [(qk32[p][:], b_qk32[p])], TTOK, rs[:], b_rs, 1.0)
                        if p == 0:
                            STT(kb, "dve", KQ[h][:, :, 1, :], qk32[0][:].rearrange("p (c t) -> p c t", c=4),
                                128.0 ** -0.5, rs[:].rearrange("p (c t) -> p c t", c=4), ALU.mult, ALU.mult,
                                [b_qk32[0], b_rs], [b_KQ[h][1]])
                        else:
                            TT(kb, "dve", kn32[:], qk32[1][:], rs[:], ALU.mult, [b_qk32[1], b_rs], [b_kn32])
                            CP(kb, "pool", KQ[h][:, :, 0, :], kn32[:].rearrange("p (c t) -> p c t", c=4),
                               [b_kn32], [b_KQ[h][0]])
                            pk, bpk = ring.next()
                            for c in range(4):
                                TR(kb, pk[:, c * 128:(c + 1) * 128], kn32[:, c * 128:(c + 1) * 128], ident, [b_kn32, b_cst], [bpk])
                            for c in range(4):
                                TS(kb, "dve", Kbg[h][c][:], pk[:, c * 128:(c + 1) * 128], beg[c][:, h:h + 1], None, ALU.mult, None,
                                   [bpk, b_tok[c]], [b_tm[h][c]])
                                TS(kb, "dve", Kdec[h][c][:], pk[:, c * 128:(c + 1) * 128], kds[c][:, h:h + 1], None, ALU.mult, None,
                                   [bpk, b_tok[c]], [b_tm[h][c]])
                    else:
                        vt_ = qk32[p - 2]; bvt_ = b_qk32[p - 2]
                        ACT(kb, vt_[:], c_[:], AF.Silu, [bc_], [bvt_])
                        pk, bpk = ring.next()
                        for c in range(4):
                            TR(kb, pk[:, c * 128:(c + 1) * 128], vt_[:, c * 128:(c + 1) * 128], ident, [bvt_, b_cst], [bpk])
                        for c in range(4):
                            TS(kb, "dve", Vb[h][c][:, (p - 2) * 128:(p - 1) * 128], pk[:, c * 128:(c + 1) * 128],
                               bsig[c][:, h:h + 1], None, ALU.mult, None, [bpk, b_tok[c]], [b_tm[h][c]])
            def pre_gen(h, c):
                si = h
                bt = b_tok[c]
                TS(kb, "dve", Gb[si][:], ones[:], gcol[c][:, h:h + 1], None, ALU.mult, None, [b_ones, bt], [b_Gb[si]])
                TS(kb, "pool", LNBb[si][:], ones[:], lnb[c][:, h:h + 1], None, ALU.mult, None, [b_ones, bt], [b_LNBb[si]])
                yield
                pb, bpb, ipb = ring.acquire()
                rd = [b_Gb[si], b_cst]
                MM(kb, pb[:, 0:128], Gb[si][:], TRI, True, True, rd, [bpb])
                MM(kb, pb[:, 128:256], Gb[si][:], TRI, True, False, rd, [bpb])
                MM(kb, pb[:, 128:256], ident, MASKI, False, True, rd, [bpb])
                MM(kb, pb[:, 256:384], Gb[si][:], TRI, True, False, rd, [bpb])
                MM(kb, pb[:, 256:384], LNBb[si][:], ident, False, False, rd + [b_LNBb[si]], [bpb])
                MM(kb, pb[:, 256:384], ident, MASKS, False, True, rd, [bpb])
                MM(kb, pb[:, 384:512], Gb[si][:], TRI, True, False, rd, [bpb])
                MM(kb, pb[:, 384:512], ident, MASKP, False, True, rd, [bpb])
                yield
                ACT(kb, EG[si][:], pb[:, 0:128], AF.Exp, [bpb], [b_EG[si]])
                ACT(kb, DT[si][:, 1, :], pb[:, 256:384], AF.Exp, [bpb, bt], [b_DT[si]], bias=ngc[c][:, h:h + 1])
                ACT(kb, DT[si][:, 2, :], pb[:, 384:512], AF.Exp, [bpb, bt], [b_DT[si]], bias=gl[c][:, h:h + 1], scale=-1.0)
                ACT(kb, DT[si][:, 0, :], pb[:, 128:256], AF.Exp, [bpb, bt], [b_DT[si]], bias=ngc[c][:, h:h + 1])
                ring.release(ipb)
                CP(kb, "pool", EGl[:, h * 4 + c:h * 4 + c + 1], EG[si][:, 127:128], [b_EG[si]], [b_pre[h][c % 2]])
                pg, bpg, ipg = ring.acquire()
                MM(kb, pg[:, 0:256], KQ[h][:, c, 0, :], KQ[h][:, c, :, :].rearrange("p a t -> p (a t)"), True, True,
                   b_KQ[h], [bpg])
                yield
                x_, xt_, r_ = XA[si], XTA[si], RA[si]
                bx_, bxt_, br_ = b_XA[si], b_XTA[si], b_RA[si]
                STT(kb, "dve", x_[0][:], pg[:, 0:128], -1.0, DT[si][:, 1, :], ALU.mult, ALU.mult, [bpg, b_DT[si]], [bx_[0]])
                STT(kb, "dve", xt_[0][:], pg[:, 0:128], -1.0, DT[si][:, 2, :], ALU.mult, ALU.mult, [bpg, b_DT[si]], [bxt_[0]])
                TT(kb, "dve", AT[h][c % 2][:], pg[:, 128:256], DT[si][:, 0, :], ALU.mult, [bpg, b_DT[si]], [b_pre[h][c % 2]])
                ring.release(ipg)
                yield
                TT(kb, "pool", r_[0][:], x_[0][:], ident, ALU.add, [bx_[0], b_cst], [br_[0]])
                cur = 0
                pc_, bpc, ipc = ring.acquire()
                MM(kb, pc_[:, 0:128], x_[0][:], xt_[0][:], True, True, [bx_[0], bxt_[0]], [bpc])
                MM(kb, pc_[:, 128:256], xt_[0][:], x_[0][:], True, True, [bx_[0], bxt_[0]], [bpc])
                yield
                for k in range(1, 7):
                    nxt = 1 - cur
                    CP(kb, "act", xt_[nxt][:], pc_[:, 0:128], [bpc], [bxt_[nxt]])
                    if k < 6:
                        CP(kb, "dve", x_[nxt][:], pc_[:, 128:256], [bpc], [bx_[nxt]])
                    yield
                    MM(kb, pc_[:, 256:384], xt_[nxt][:], r_[cur][:], True, True, [bxt_[nxt], br_[cur]], [bpc])
                    if k < 6:
                        MM(kb, pc_[:, 0:128], x_[nxt][:], xt_[nxt][:], True, True, [bx_[nxt], bxt_[nxt]], [bpc])
                        if k < 5:
                            MM(kb, pc_[:, 128:256], xt_[nxt][:], x_[nxt][:], True, True, [bx_[nxt], bxt_[nxt]], [bpc])
                    yield
                    TT(kb, "dve", r_[nxt][:], pc_[:, 256:384], r_[cur][:], ALU.add, [bpc, br_[cur]], [br_[nxt]])
                    cur = nxt
                ring.release(ipc)
                CP(kb, "act", Rb[si][:], r_[cur][:], [br_[cur]], [b_Rb[si]])
                R_ = Rb[si]; bR_ = b_Rb[si]
                TT(kb, "pool", QgT[h][c % 2][:], KQ[h][:, c, 1, :], EG[si][:], ALU.mult, [b_KQ[h][1], b_EG[si]], [b_pre[h][c % 2]])
                yield
                pu, bpu, ipu = ring.acquire()
                MM(kb, pu[:, 0:256], R_[:], Vb[h][c][:], True, True, [bR_, b_tm[h][c]], [bpu])
                MM(kb, pu[:, 256:384], Kbg[h][c][:], R_[:], True, True, [bR_, b_tm[h][c]], [bpu])
                yield
                CP(kb, "act", U[h][c % 2][:], pu[:, 0:256], [bpu], [b_pre[h][c % 2]])
                TS(kb, "dve", nWT[h][c % 2][:], pu[:, 256:384], -1.0, None, ALU.mult, None, [bpu], [b_pre[h][c % 2]])
                ring.release(ipu)

            def seq_gen(h, c, ob):
                pw, bpw, ipw = ring.acquire()
                MM(kb, pw[:, 0:256], nWT[h][c % 2][:], Sst[h][:], True, True, [b_pre[h][c % 2], b_S[h]], [bpw])
                yield
                TT(kb, "dve", Vn[h][:], pw[:, 0:256], U[h][c % 2][:], ALU.add, [bpw, b_pre[h][c % 2]], [b_Vn[h]])
                ring.release(ipw)
                yield
                pq, bpq, ipq = ring.acquire()
                MM(kb, pq[:, 0:256], QgT[h][c % 2][:], Sst[h][:], True, False, [b_pre[h][c % 2], b_S[h]], [bpq])
                MM(kb, pq[:, 0:256], AT[h][c % 2][:], Vn[h][:], False, True, [b_pre[h][c % 2], b_Vn[h]], [bpq])
                MM(kb, pq[:, 256:512], Kdec[h][c][:], Vn[h][:], True, True, [b_tm[h][c], b_Vn[h]], [bpq])
                yield
                STT(kb, "dve", Sst[h][:], Sst[h][:], EGl[:, h * 4 + c:h * 4 + c + 1], pq[:, 256:512], ALU.mult, ALU.add,
                    [b_S[h], b_pre[h][c % 2], bpq], [b_S[h]])
                CP(kb, "act", o32[h][:], pq[:, 0:256], [bpq], [b_o32[h]])
                ring.release(ipq)
                yield
                ACT(kb, osq[h][:], o32[h][:], AF.Square, [b_o32[h]], [b_osq[h]])
                yield
                kb.op("dve", lambda e: e.reduce_sum(oss[h][:], osq[h][:], axis=AX.X), [b_osq[h]], [b_oss[h]])
                TS(kb, "dve", oss[h][:], oss[h][:], 1.0 / 256, EPS, ALU.mult, ALU.add, [b_oss[h]], [b_oss[h]])
                RECIP(kb, oss[h][:], oss[h][:], [b_oss[h]], [b_oss[h]])
                yield
                ACT(kb, oss[h][:], oss[h][:], AF.Sqrt, [b_oss[h]], [b_oss[h]])
                yield
                STT(kb, "dve", osq[h][:], o32[h][:], oss[h][:, 0:1], ngb[:], ALU.mult, ALU.mult, [b_o32[h], b_oss[h], b_ng], [b_osq[h]])
                TT(kb, "dve", ogt[ob][:, h * 256:(h + 1) * 256], osq[h][:], sgate[c][:, h * 256:(h + 1) * 256], ALU.mult,
                   [b_osq[h], b_sgate[c]], [b_ogt[ob]])

            def emit_out(c, ob):
                for hb in range(2):
                    pt_, bpt_ = ring.next()
                    for q_ in range(4):
                        TR(kb, pt_[:, q_ * 128:(q_ + 1) * 128], ogt[ob][:, (hb * 4 + q_) * 128:(hb * 4 + q_ + 1) * 128], ident,
                           [b_ogt[ob], b_cst], [bpt_])
                    CP(kb, "act", ogT[ob][:, hb * 4:(hb + 1) * 4, :], pt_[:].rearrange("p (q t) -> p q t", q=4), [bpt_], [b_ogT[ob]])
                kb.dma("sp", D.out_h[T // 4][T % 4].rearrange("(blk p) t -> p blk t", p=128)[:, :, c * 128:(c + 1) * 128], ogT[ob][:],
                       reads=[b_ogT[ob]], writes=[D.b_out[T // 4][T % 4]])

            def run_pool(gens):
                gens = list(gens)
                while gens:
                    for g_ in list(gens):
                        try:
                            next(g_)
                        except StopIteration:
                            gens.remove(g_)

            run_pool([pre_gen(h, 0) for h in range(GH)])
            prev_out = None
            for c in range(4):
                ob = ogcnt % 2
                ogcnt += 1
                pool_ = [seq_gen(h, c, ob) for h in range(GH)]
                if c < 3:
                    pool_ += [pre_gen(h, c + 1) for h in range(GH)]
                run_pool(pool_)
                emit_out(c, ob)
            if D.after_tile is not None:
                D.after_tile(T)
        kb.barrier()
        kb.flush()
    kb.tstack = kb.stack


def _gdn_consts():
    j = np.arange(128)[:, None]
    i = np.arange(128)[None, :]
    ident = (j == i).astype(np.float32)
    tri = (j <= i).astype(np.float32)
    maski = np.where(j <= i, 0.0, NEG).astype(np.float32)
    masks = np.where(j < i, 0.0, NEG).astype(np.float32)
    maskp = np.where(j <= i, -NEG, 0.0).astype(np.float32)
    return np.concatenate([ident, tri, maski, masks, maskp], axis=1)


MH = 8
C1_2PI = 6.28125
C2_2PI = 2.0 * np.pi - 6.28125


def emit_mla(nc, kb, D, ntile=8):
    S = ntile * TTOK
    NW = 1408 + 1024
    with contextlib.ExitStack() as st_:
        kb.tstack = st_
        kb.pfx = D.pfx
        ring = PsumRing(kb, 4)
        acc = [kb.psum(f"acc{i}", [128, 512], F32) for i in range(4)]
        b_acc = kb.bufs(4, "acc")
        for b in b_acc:
            b.excl = True
        ones = kb.sbuf("ones", [128, 128], F32); b_ones = kb.buf("ones")
        MEMSET(kb, "pool", ones[:], 1.0, [b_ones])
        onesb = kb.sbuf("onesb", [128, 128], BF16); b_onesb = kb.buf("onesb")
        MEMSET(kb, "pool", onesb[:], 1.0, [b_onesb])
        cst = kb.sbuf("cst", [128, 512], F32); b_cst = kb.buf("cst")
        kb.dma("sp", cst[:], D.cst_d, writes=[b_cst])
        cstb = kb.sbuf("cstb", [128, 384], BF16); b_cstb = kb.buf("cstb")
        CP(kb, "dve", cstb[:], cst[:, 0:384], [b_cst], [b_cstb])
        identb, foldb, masktri = cstb[:, 0:128], cstb[:, 128:256], cstb[:, 256:384]
        inv_col = cst[:, 384:385]
        sgn_col = cst[:, 385:386]
        phs_col = cst[:, 386:387]
        gsb = kb.sbuf("gsb", [128, 8], F32); b_g = kb.buf("g")
        kb.dma("sp", gsb[:], D.g_d, writes=[b_g])
        gq = kb.sbuf("gq", [128, 6], F32); b_gq = kb.buf("gq")
        kb.dma("sp", gq[:], D.gq_d, writes=[b_gq])
        gkv = kb.sbuf("gkv", [128, 4], F32); b_gkv = kb.buf("gkv")
        kb.dma("sp", gkv[:], D.gkv_d, writes=[b_gkv])

        arena = kb.sbuf("arena", [128, 8 * NW], BF16); b_ar = kb.bufs(2, "arena")
        Win = arena[:, :].rearrange("p (kc n) -> p kc n", kc=8)
        Wuq = kb.sbuf("Wuq", [128, 6, MH * 256], BF16); b_Wuq = kb.buf("Wuq")
        Wukv = kb.sbuf("Wukv", [128, 4, 2048], BF16); b_Wukv = kb.buf("Wukv")
        winv = D.win_d.rearrange("(kc p) n -> p kc n", p=128)
        for kc in range(8):
            kb.dma("pool", Win[:, kc, :], winv[:, kc, :], writes=b_ar)
        wuqv = D.wuq_d.rearrange("(kc p) n -> p kc n", p=128)
        for kc in range(6):
            kb.dma("pool", Wuq[:, kc, :], wuqv[:, kc, :], writes=[b_Wuq])
        kb.dma("pool", Wukv[:], D.wukv_d.rearrange("(kc p) n -> p kc n", p=128), writes=[b_Wukv])

        xt = kb.sbuf("xt", [128, 8, TTOK], F32); b_xt = kb.bufs(8, "xt")
        hT = kb.sbuf("hT", [128, 8, TTOK], BF16); b_h = kb.buf("hT")
        sq_t = [kb.sbuf(f"sq{i}", [128, TTOK], BF16) for i in range(2)]; b_sq = kb.bufs(2, "sq")
        rstd = kb.sbuf("rstd", [128, TTOK], F32); b_rstd = kb.buf("rstd")
        cq32 = kb.sbuf("cq32", [128, 6, TTOK], F32); b_cq32 = kb.bufs(6, "cq32")
        ckv32 = kb.sbuf("ckv32", [128, 4, TTOK], F32); b_ckv32 = kb.bufs(4, "ckv32")
        cqn = kb.sbuf("cqn", [128, 6, TTOK], BF16); b_cqn = kb.buf("cqn")
        ckvn = kb.sbuf("ckvn", [128, 4, TTOK], BF16); b_ckvn = kb.buf("ckvn")
        posi = kb.sbuf("posi", [128, TTOK], I32); b_posi = kb.buf("posi")
        ang = kb.sbuf("ang", [128, TTOK], F32); b_ang = kb.buf("ang")
        kq = kb.sbuf("kq", [128, TTOK], F32); b_kq = kb.buf("kq")
        kqi = kb.sbuf("kqi", [128, TTOK], I32); b_kqi = kb.buf("kqi")
        tabs = kb.sbuf("tabs", [128, TTOK], F32); b_tabs = kb.buf("tabs")
        kprod = kb.sbuf("kprod", [128, TTOK], BF16); b_kprod = kb.buf("kprod")
        krd = kb.sbuf("krd", [128, S], BF16); b_krd = kb.bufs(ntile, "krd")
        stg = [kb.sbuf(f"stg{i}", [128, TTOK], BF16) for i in range(4)]; b_stg = kb.bufs(4, "stg")
        qstg = [kb.sbuf(f"qstg{i}", [128, 2, TTOK], BF16) for i in range(2)]; b_qstg = kb.bufs(2, "qstg")
        vstg = [kb.sbuf(f"vstg{i}", [128, 1024], BF16) for i in range(2)]; b_vstg = kb.bufs(2, "vstg")
        b_qs = [[kb.buf(f"qs{h}_{t}") for t in range(ntile)] for h in range(MH)]
        b_kn = [kb.buf(f"kn{h}") for h in range(MH)]
        b_v = kb.buf("vs")
        b_sg = [[kb.buf(f"sg{h}_{t}") for t in range(ntile)] for h in range(MH)]

        sc = 0
        for T in range(ntile):
            t0 = T * TTOK
            ts_ = slice(t0, t0 + TTOK)
            for kc in range(8):
                kb.dma("sp", xt[:, kc, :], D.xtile(T, kc), reads=D.x_reads, writes=[b_xt[kc]])
            kb.dma("sp", posi[:], D.pos_d[:, ts_], writes=[b_posi])
            rms_stats(kb, ring, onesb[:], b_onesb, sq_t, b_sq, [(xt[:, kc, :], b_xt[kc]) for kc in range(8)],
                      TTOK, rstd[:], b_rstd, 1.0 / 1024)
            for kc in range(8):
                STT(kb, "dve", hT[:, kc, :], xt[:, kc, :], gsb[:, kc:kc + 1], rstd[:], ALU.mult, ALU.mult,
                    [b_xt[kc], b_g, b_rstd], [b_h])
            CP(kb, "dve", ang[:], posi[:], [b_posi], [b_ang])
            TS(kb, "dve", ang[:], ang[:], inv_col, phs_col, ALU.mult, ALU.add, [b_ang, b_cst], [b_ang])
            TS(kb, "dve", kq[:], ang[:], 1.0 / (2 * np.pi), None, ALU.mult, None, [b_ang], [b_kq])
            CP(kb, "dve", kqi[:], kq[:], [b_kq], [b_kqi])
            CP(kb, "dve", kq[:], kqi[:], [b_kqi], [b_kq])
            STT(kb, "dve", ang[:], kq[:], -C1_2PI, ang[:], ALU.mult, ALU.add, [b_kq, b_ang], [b_ang])
            STT(kb, "dve", ang[:], kq[:], -C2_2PI, ang[:], ALU.mult, ALU.add, [b_kq, b_ang], [b_ang])
            TS(kb, "dve", ang[:], ang[:], 3.14159, -3.14159, ALU.min, ALU.max, [b_ang], [b_ang])
            ACT(kb, tabs[:], ang[:], AF.Sin, [b_ang], [b_tabs])
            TS(kb, "dve", tabs[:], tabs[:], sgn_col, None, ALU.mult, None, [b_tabs, b_cst], [b_tabs])
            for oc in range(11):
                ps, bps = ring.next()
                for kc in range(8):
                    MM(kb, ps[:], Win[:, kc, oc * 128:(oc + 1) * 128], hT[:, kc, :], kc == 0, kc == 7, b_ar + [b_h], [bps])
                if oc < 6:
                    CP(kb, "act", cq32[:, oc, :], ps[:], [bps], [b_cq32[oc]])
                elif oc < 10:
                    CP(kb, "act", ckv32[:, oc - 6, :], ps[:], [bps], [b_ckv32[oc - 6]])
                else:
                    TT(kb, "dve", kprod[:], ps[:], tabs[:], ALU.mult, [bps, b_tabs], [b_kprod])
                    ps2, bps2 = ring.next()
                    MM(kb, ps2[:], foldb, kprod[:], True, True, [b_cstb, b_kprod], [bps2])
                    CP(kb, "act", krd[:, ts_], ps2[:], [bps2], [b_krd[T]])
            rms_stats(kb, ring, onesb[:], b_onesb, sq_t, b_sq, [(cq32[:, i, :], b_cq32[i]) for i in range(6)],
                      TTOK, rstd[:], b_rstd, 1.0 / 768)
            for i in range(6):
                STT(kb, "dve", cqn[:, i, :], cq32[:, i, :], gq[:, i:i + 1], rstd[:], ALU.mult, ALU.mult,
                    [b_cq32[i], b_gq, b_rstd], [b_cqn])
            rms_stats(kb, ring, onesb[:], b_onesb, sq_t, b_sq, [(ckv32[:, i, :], b_ckv32[i]) for i in range(4)],
                      TTOK, rstd[:], b_rstd, 1.0 / 512)
            for i in range(4):
                STT(kb, "dve", ckvn[:, i, :], ckv32[:, i, :], gkv[:, i:i + 1], rstd[:], ALU.mult, ALU.mult,
                    [b_ckv32[i], b_gkv, b_rstd], [b_ckvn])
            for h in range(MH):
                ps, bps = ring.next()
                for kc in range(8):
                    MM(kb, ps[:], Win[:, kc, 1408 + h * 128:1408 + (h + 1) * 128], hT[:, kc, :], kc == 0, kc == 7,
                       b_ar + [b_h], [bps])
                s_ = sc % 4; sc += 1
                ACT(kb, stg[s_][:], ps[:], AF.Silu, [bps], [b_stg[s_]])
                kb.dma("sp", D.sg_d[h * 128:(h + 1) * 128, ts_], stg[s_][:], reads=[b_stg[s_]], writes=[b_sg[h][T]])
            for h in range(MH):
                qi = (T * MH + h) % 2
                for part in range(2):
                    ps, bps = ring.next()
                    col = h * 256 + part * 128
                    for kc in range(6):
                        MM(kb, ps[:], Wuq[:, kc, col:col + 128], cqn[:, kc, :], kc == 0, kc == 5, [b_Wuq, b_cqn], [bps])
                    if part == 0:
                        CP(kb, "act", qstg[qi][:, 0, :], ps[:], [bps], [b_qstg[qi]])
                    else:
                        TT(kb, "dve", qstg[qi][:, 1, :], ps[:], tabs[:], ALU.mult, [bps, b_tabs], [b_qstg[qi]])
                kb.dma("sp", D.qs_d[h, :, :, ts_], qstg[qi][:], reads=[b_qstg[qi]], writes=[b_qs[h][T]])
            for h in range(MH):
                ps, bps = ring.next()
                for kc in range(4):
                    MM(kb, ps[:], Wukv[:, kc, h * 128:(h + 1) * 128], ckvn[:, kc, :], kc == 0, kc == 3, [b_Wukv, b_ckvn], [bps])
                s_ = sc % 4; sc += 1
                CP(kb, "act", stg[s_][:], ps[:], [bps], [b_stg[s_]])
                kb.dma("sp", D.kn_d[h, :, ts_], stg[s_][:], reads=[b_stg[s_]], writes=[b_kn[h]])
            for blk in range(4):
                vi = (T * 4 + blk) % 2
                for half in range(2):
                    ps, bps = ring.next()
                    for kc in range(4):
                        MM(kb, ps[:], ckvn[:, kc, blk * 128:(blk + 1) * 128], Wukv[:, kc, 1024 + half * 512:1024 + (half + 1) * 512],
                           kc == 0, kc == 3, [b_Wukv, b_ckvn], [bps])
                    CP(kb, "act", vstg[vi][:, half * 512:(half + 1) * 512], ps[:], [bps], [b_vstg[vi]])
                kb.dma("sp", D.v_d[t0 + blk * 128:t0 + (blk + 1) * 128, :], vstg[vi][:], reads=[b_vstg[vi]], writes=[b_v])

        NKT = S // 128
        KV = [arena[:, i * 8192:(i + 1) * 8192] for i in range(2)]
        qt_ = [kb.sbuf(f"qt{i}", [128, 2, TTOK], BF16) for i in range(2)]; b_qt = kb.bufs(2, "qt")
        sgt = [kb.sbuf(f"sgt{i}", [128, TTOK], BF16) for i in range(2)]; b_sgt = kb.bufs(2, "sgt")
        pT = [kb.sbuf(f"pT{i}", [128, TTOK], BF16) for i in range(5)]; b_pT = kb.bufs(5, "pT")
        rcp = kb.sbuf("rcp", [128, TTOK], F32); b_rcp = kb.buf("rcp")
        o32 = kb.sbuf("o32", [128, TTOK], F32); b_o32 = kb.buf("o32")
        ob = [kb.sbuf(f"ob{i}", [128, TTOK], BF16) for i in range(2)]; b_ob = kb.bufs(2, "ob")
        scale = 192.0 ** -0.5
        nb8 = kb.sbuf("nb8", [128, 1], F32); b_nb8 = kb.buf("nb8")
        MEMSET(kb, "pool", nb8[:], -8.0, [b_nb8])
        vvw = D.v_d.rearrange("(kt p) d -> p kt d", p=128)
        DEPTH = 3
        items = []
        for h in range(MH):
            for qt in range(ntile):
                nkt = 4 * qt + 4
                for kt in range(nkt):
                    items.append((h, qt, kt, nkt))
        state = {"pc": 0, "it": -1}
        grp = {}

        def start_head(h):
            kvb = h % 2
            Kh = KV[kvb][:, 0:4096]
            Vh = KV[kvb][:, 4096:8192].rearrange("p (kt d) -> p kt d", d=128)
            kb.dma("sp", Kh[:, 0:S], D.kn_d[h], reads=[b_kn[h]], writes=[b_ar[kvb]])
            for g4 in range(0, NKT, 8):
                kb.dma("sp", Vh[:, g4:g4 + 8, :], vvw[:, g4:g4 + 8, h * 128:(h + 1) * 128], reads=[b_v], writes=[b_ar[kvb]])

        def start_group(h, qt):
            state["it"] += 1
            qi = state["it"] % 2
            q0 = qt * TTOK
            kb.dma("sp", qt_[qi][:], D.qs_d[h, :, :, q0:q0 + TTOK], reads=[b_qs[h][qt]], writes=[b_qt[qi]])
            kb.dma("sp", sgt[qi][:], D.sg_d[h * 128:(h + 1) * 128, q0:q0 + TTOK], reads=[b_sg[h][qt]], writes=[b_sgt[qi]])
            grp[(h, qt)] = qi

        def emit_S(h, qt, kt, nkt):
            if kt == 0:
                if qt == 0:
                    start_head(h)
                start_group(h, qt)
            qi = grp[(h, qt)]
            kvb = h % 2
            Kh = KV[kvb][:, 0:4096]
            j = kt - 4 * qt
            c0 = 0 if j < 0 else j * 128
            ks = slice(kt * 128, (kt + 1) * 128)
            ps, bps = ring.next()
            MM(kb, ps[:, c0:TTOK], Kh[:, ks], qt_[qi][:, 0, c0:TTOK], True, False, [b_ar[kvb], b_qt[qi]], [bps])
            MM(kb, ps[:, c0:TTOK], krd[:, ks], qt_[qi][:, 1, c0:TTOK], False, j < 0, [b_krd[kt // 4], b_qt[qi]], [bps])
            if j >= 0:
                MM(kb, ps[:, c0:c0 + 128], identb, masktri, False, True, [b_cstb], [bps])
            pi_ = state["pc"] % len(pT); state["pc"] += 1
            ACT(kb, pT[pi_][:, c0:TTOK], ps[:, c0:TTOK], AF.Exp, [bps, b_nb8], [b_pT[pi_]], bias=nb8[:, 0:1], scale=scale)
            return (h, qt, kt, nkt, pi_, c0)

        def emit_PV(h, qt, kt, nkt, pi_, c0):
            qi = grp[(h, qt)]
            kvb = h % 2
            Vh = KV[kvb][:, 4096:8192].rearrange("p (kt d) -> p kt d", d=128)
            ao, asum = acc[qi * 2], acc[qi * 2 + 1]
            bao, basum = b_acc[qi * 2], b_acc[qi * 2 + 1]
            MM(kb, ao[:, c0:TTOK], Vh[:, kt, :], pT[pi_][:, c0:TTOK], kt == 0, kt == nkt - 1, [b_ar[kvb], b_pT[pi_]], [bao])
            MM(kb, asum[:, c0:TTOK], onesb[:], pT[pi_][:, c0:TTOK], kt == 0, kt == nkt - 1, [b_onesb, b_pT[pi_]], [basum])
            if kt == nkt - 1:
                RECIP(kb, rcp[:], asum[:], [basum], [b_rcp])
                TT(kb, "dve", o32[:], ao[:], rcp[:], ALU.mult, [bao, b_rcp], [b_o32])
                TT(kb, "dve", ob[qi][:], o32[:], sgt[qi][:], ALU.mult, [b_o32, b_sgt[qi]], [b_ob[qi]])
                kb.dma("sp", D.out_h[qt // 4][qt % 4][h * 128:(h + 1) * 128, :], ob[qi][:],
                       reads=[b_ob[qi]], writes=[D.b_out[qt // 4][qt % 4]])

        pending = []
        for itx in items:
            pending.append(emit_S(*itx))
            if len(pending) > DEPTH:
                emit_PV(*pending.pop(0))
        while pending:
            emit_PV(*pending.pop(0))
        kb.barrier()
        kb.flush()
    kb.tstack = kb.stack


def _mla_consts():
    j = np.arange(128)[:, None]
    i = np.arange(128)[None, :]
    ident = (j == i).astype(np.float32)
    fold = ((j % 64) == (i % 64)).astype(np.float32)
    masktri = np.where(j <= i, 0.0, NEG).astype(np.float32)
    last = np.zeros((128, 128), np.float32)
    p = np.arange(128)
    last[:, 0] = (10000.0 ** (-(p % 32).astype(np.float64) / 32.0)).astype(np.float32)
    last[:, 1] = np.where(p < 64, 1.0, np.where(p < 96, -1.0, 1.0))
    last[:, 2] = np.where(p < 64, np.pi / 2, 0.0)
    return np.concatenate([ident, fold, masktri, last], axis=1).astype(np.float32)


PAIRS = [[0, 1], [2, 3], [4, 5], [6, 7]]


def build_mega():
    nc = bass.Bass("TRN2", target_bir_lowering=False)
    W = NT + HAL

    def ext(name, shape, dtype=F32):
        return nc.dram_tensor(name, list(shape), dtype, kind="ExternalInput").ap()

    def scr(name, shape, dtype):
        return nc.dram_tensor(name, list(shape), dtype).ap()

    xT_d = ext("xT", [1024, W])
    flag_d = ext("flag", [128, 2])
    invc_d = ext("invcnt", [128, 64])
    P = []
    for i in range(2):
        p = NS()
        p.g_d = ext(f"p{i}_g", [128, 8]); p.win_d = ext(f"p{i}_w_in", [1024, 4096]); p.wgrp_d = ext(f"p{i}_w_grp", [4, 512, 512])
        p.wout_d = ext(f"p{i}_w_out", [2048, 1024]); p.scale_d = ext(f"p{i}_scale", [128, 16])
        P.append(p)
    fg_d = ext("fg", [128, 8])
    G = NS()
    G.pfx = "s2_"
    G.g_d = ext("gd_g", [128, 8]); G.wf_d = ext("gd_wf", [1024, GH * 512]); G.wg_d = ext("gd_wg", [1024, GH * 256])
    G.wba_d = ext("gd_wba", [1024, 8]); G.cw_d = ext("gd_convw", [128, 64]); G.hp_d = ext("gd_hp", [128, 8])
    G.ng_d = ext("gd_ng", [128, 256]); G.cst_d = ext("gd_cst", [128, 5 * 128])
    gd_wout = ext("gd_w_out", [2048, 1024])
    M = NS()
    M.pfx = "s4_"
    M.g_d = ext("ml_g", [128, 8]); M.win_d = ext("ml_w_in", [1024, 1408 + 1024]); M.wuq_d = ext("ml_w_uq", [768, MH * 256])
    M.wukv_d = ext("ml_w_ukv", [512, 2048]); M.gq_d = ext("ml_gq", [128, 6]); M.gkv_d = ext("ml_gkv", [128, 4])
    M.pos_d = ext("ml_pos", [128, SEQ], I32); M.cst_d = ext("ml_cst", [128, 4 * 128])
    ml_wout = ext("ml_w_out", [2048, 1024])
    out_d = nc.dram_tensor("out", [1024, NT], F32, kind="ExternalOutput").ap()

    x1t = [scr(f"x1t{t}", [1024, TTOK], F32) for t in range(4)]; x1g = [scr(f"x1g{t}", [2048, TTOK], F32) for t in range(4)]
    x2t = [scr(f"x2t{t}", [1024, TTOK], F32) for t in range(4)]; x2g = [scr(f"x2g{t}", [2048, TTOK], F32) for t in range(4)]
    ogh = [[scr(f"ogh{i}_{t}", [1024, TTOK], BF16) for t in range(4)] for i in range(2)]
    ogg = [[scr(f"ogg{i}_{t}", [2048, TTOK], BF16) for t in range(4)] for i in range(2)]
    oah = [[scr(f"oah{i}_{t}", [1024, TTOK], BF16) for t in range(4)] for i in range(2)]
    oag = [[scr(f"oag{i}_{t}", [2048, TTOK], BF16) for t in range(4)] for i in range(2)]
    M.qs_d = scr("ml_qs", [MH, 128, 2, SEQ], BF16); M.kn_d = scr("ml_kn", [MH, 128, SEQ], BF16)
    M.v_d = scr("ml_vs", [SEQ, MH * 128], BF16); M.sg_d = scr("ml_sgs", [MH * 128, SEQ], BF16)

    with contextlib.ExitStack() as st:
        kb = KB(nc, st)
        b_x1t, b_x1g, b_x2t, b_x2g = kb.bufs(4, "x1t"), kb.bufs(4, "x1g"), kb.bufs(4, "x2t"), kb.bufs(4, "x2g")
        b_ogh = [kb.bufs(4, f"ogh{i}_") for i in range(2)]; b_ogg = [kb.bufs(4, f"ogg{i}_") for i in range(2)]
        b_oah = [kb.bufs(4, f"oah{i}_") for i in range(2)]; b_oag = [kb.bufs(4, f"oag{i}_") for i in range(2)]

        def xtile_from(xg):
            return lambda T, kc: xg[T % 4][(T // 4) * 1024 + kc * 128:(T // 4) * 1024 + (kc + 1) * 128, :]

        D = NS(); D.pfx = "s1_"; D.flag_d = flag_d
        D.x_main = [xT_d[:, HAL + t * TTOK:HAL + (t + 1) * TTOK] for t in range(4)]; D.x_reads = []
        D.x_halo = xT_d[:, 0:HAL]; D.xh_reads = []; D.halo_flag = False
        D.g_d, D.win_d, D.wgrp_d, D.wout_d, D.scale_d, D.invc_d = P[0].g_d, P[0].win_d, P[0].wgrp_d, P[0].wout_d, P[0].scale_d, invc_d
        D.out = x1t; D.out_is_output = False; D.b_out = b_x1t
        D.after_tile = lambda t: kb.allgather(x1g[t], x1t[t], PAIRS, reads=[b_x1t[t]], writes=[b_x1g[t]])
        emit_tok(nc, kb, D, False, True, False)
        G.xtile = xtile_from(x1g); G.x_reads = b_x1g; G.out_h = ogh; G.b_out = b_ogh
        G.after_tile = lambda T: kb.allgather(ogg[T // 4][T % 4], ogh[T // 4][T % 4], PAIRS,
                                              reads=[b_ogh[T // 4][T % 4]], writes=[b_ogg[T // 4][T % 4]])
        emit_gdn(nc, kb, G, SEQ // TTOK)
        D = NS(); D.pfx = "s3_"; D.flag_d = flag_d
        D.x_main = x1t; D.x_reads = b_x1t; D.x_halo = None; D.halo_flag = False
        D.y0, D.y1, D.y_reads, D.y_halo = ogg[0], ogg[1], b_ogg[0] + b_ogg[1], None
        D.wprev_d = gd_wout
        D.out = x2t; D.out_is_output = False; D.b_out = b_x2t
        D.after_tile = lambda t: kb.allgather(x2g[t], x2t[t], PAIRS, reads=[b_x2t[t]], writes=[b_x2g[t]])
        emit_tok(nc, kb, D, True, False, False)
        M.xtile = xtile_from(x2g); M.x_reads = b_x2g; M.out_h = oah; M.b_out = b_oah
        emit_mla(nc, kb, M, SEQ // TTOK)
        for i in range(2):
            for t in range(4):
                kb.allgather(oag[i][t], oah[i][t], PAIRS, reads=[b_oah[i][t]], writes=[b_oag[i][t]])
        D = NS(); D.pfx = "s5_"; D.flag_d = flag_d
        D.x_main = x2t; D.x_reads = b_x2t; D.x_halo = x2g[3][0:1024, TTOK - HAL:TTOK]; D.xh_reads = b_x2g; D.halo_flag = True
        D.y0, D.y1, D.y_reads, D.y_halo = oag[0], oag[1], b_oag[0] + b_oag[1], oag[0][3][:, TTOK - HAL:TTOK]
        D.wprev_d = ml_wout
        D.g_d, D.win_d, D.wgrp_d, D.wout_d, D.scale_d, D.invc_d = P[1].g_d, P[1].win_d, P[1].wgrp_d, P[1].wout_d, P[1].scale_d, invc_d
        D.fg_d = fg_d
        D.out = [out_d[:, t * TTOK:(t + 1) * TTOK] for t in range(4)]; D.out_is_output = True; D.b_out = None
        D.after_tile = None
        emit_tok(nc, kb, D, True, True, True)
        kb.finish()
    return nc


_NC = {}


def _gdn_host(c, w_in, conv_w, a_log, dt_bias, norm_g, g):
    hg = c % 2
    heads = [hg * GH + i for i in range(GH)]
    cols, cwl = [], []
    for h in heads:
        for (base, n) in ((h * 128, 128), (1024 + h * 128, 128), (2048 + h * 256, 128), (2048 + h * 256 + 128, 128)):
            cols.extend(range(base, base + n))
            cwl.append(conv_w[:, base:base + n].T)
    gcols = []
    for h in heads:
        gcols.extend(range(4096 + h * 256, 4096 + (h + 1) * 256))
    bacols = [6144 + h for h in heads] + [6152 + h for h in heads]
    hpv = np.concatenate([a_log[heads], dt_bias[heads]]).astype(np.float32)
    return {
        "gd_g": _colmajor128(g, 8),
        "gd_wf": np.ascontiguousarray(w_in[:, cols]),
        "gd_wg": np.ascontiguousarray(w_in[:, gcols]),
        "gd_wba": np.ascontiguousarray(w_in[:, bacols]),
        "gd_convw": np.ascontiguousarray(np.concatenate(cwl, axis=1), dtype=np.float32),
        "gd_hp": np.ascontiguousarray(np.broadcast_to(hpv[None, :], (128, 8))),
        "gd_ng": np.ascontiguousarray(np.broadcast_to(np.asarray(norm_g, np.float32)[None, :], (128, 256))),
        "gd_cst": _gdn_consts(),
    }


def _mla_host(c, positions_b, g, w_in, gq, w_uq, gkv, w_ukv):
    hg = c % 2
    sw = list(range(32, 64)) + list(range(0, 32))
    heads = [hg * MH + i for i in range(MH)]
    kr0 = 768 + 512
    cols = list(range(0, kr0 + 64)) + [kr0 + s_ for s_ in sw]
    for h in heads:
        cols.extend(range(kr0 + 64 + h * 128, kr0 + 64 + (h + 1) * 128))
    qcols = []
    for h in heads:
        base = h * 192
        qcols.extend(range(base, base + 192))
        qcols.extend([base + 128 + s_ for s_ in sw])
    kvcols = []
    for h in heads:
        kvcols.extend(range(h * 256, h * 256 + 128))
    for h in heads:
        kvcols.extend(range(h * 256 + 128, h * 256 + 256))
    return {
        "ml_g": _colmajor128(g, 8),
        "ml_w_in": np.ascontiguousarray(w_in[:, cols]),
        "ml_w_uq": np.ascontiguousarray(w_uq[:, qcols]),
        "ml_w_ukv": np.ascontiguousarray(w_ukv[:, kvcols]),
        "ml_gq": _colmajor128(gq, 6),
        "ml_gkv": _colmajor128(gkv, 4),
        "ml_pos": np.ascontiguousarray(np.broadcast_to(positions_b.astype(np.int32)[None, :], (128, SEQ))),
        "ml_cst": _mla_consts(),
    }


def kernel(x, positions, norm_g, pool_w_in, pool_w_grp, pool_scale, pool_w_out,
           gdn_w_in, gdn_conv, gdn_a_log, gdn_dt_bias, gdn_norm_g, gdn_w_out,
           mla_w_in, mla_q_norm_g, mla_w_uq, mla_kv_norm_g, mla_w_ukv, mla_w_out, final_g):
    f = lambda a: np.ascontiguousarray(np.asarray(a, dtype=np.float32))
    x = f(x); norm_g = f(norm_g); positions = np.asarray(positions)
    pool_w_in, pool_w_grp, pool_scale, pool_w_out = f(pool_w_in), f(pool_w_grp), f(pool_scale), f(pool_w_out)
    gdn_w_in, gdn_conv = f(gdn_w_in)[0], f(gdn_conv)[0]
    mla_w_in, mla_w_uq, mla_w_ukv = f(mla_w_in)[0], f(mla_w_uq)[0], f(mla_w_ukv)[0]
    if "mega" not in _NC:
        _NC["mega"] = build_mega()
    nc = _NC["mega"]
    in_maps = []
    for c in range(NCORES):
        b, r = c // 2, c % 2
        m = {"xT": _halo_T(x[b], r),
             "flag": np.ascontiguousarray(np.broadcast_to(np.array([[float(r), 1.0 - float(r)]], np.float32), (128, 2))),
             "invcnt": _invcnt(r)}
        for i, li in enumerate((0, 3)):
            m[f"p{i}_g"] = _colmajor128(norm_g[li], 8)
            m[f"p{i}_w_in"] = pool_w_in[i]
            m[f"p{i}_w_grp"] = pool_w_grp[i]
            m[f"p{i}_w_out"] = pool_w_out[i]
            m[f"p{i}_scale"] = _colmajor128(pool_scale[i], 16)
        m["fg"] = _colmajor128(f(final_g), 8)
        m.update(_gdn_host(c, gdn_w_in, gdn_conv, f(gdn_a_log)[0], f(gdn_dt_bias)[0], f(gdn_norm_g)[0], norm_g[1]))
        m["gd_w_out"] = f(gdn_w_out)[0]
        m.update(_mla_host(c, positions[b], norm_g[2], mla_w_in, f(mla_q_norm_g)[0], mla_w_uq, f(mla_kv_norm_g)[0], mla_w_ukv))
        m["ml_w_out"] = f(mla_w_out)[0]
        in_maps.append(m)
    res = run_bass_kernel_spmd(nc, in_maps, core_ids=list(range(NCORES)))
    out = np.empty(x.shape, np.float32)
    for c in range(NCORES):
        b, r = c // 2, c % 2
        out[b, r * NT:(r + 1) * NT] = res.results[c]["out"].T
    return out
```

```python
import contextlib
import numpy as np
import ml_dtypes
import concourse.bass as bass
import concourse.mybir as mybir
from concourse.bass_utils import run_bass_kernel_spmd

F32 = mybir.dt.float32
BF16 = mybir.dt.bfloat16
I32 = mybir.dt.int32
AF = mybir.ActivationFunctionType
ALU = mybir.AluOpType
AX = mybir.AxisListType

EPS = 1e-6
NCORES = 8


class Buf:
    __slots__ = ("name", "writer", "readers", "excl")

    def __init__(self, name):
        self.name = name
        self.excl = False
        self.writer = None
        self.readers = []


class _Eng:
    def __init__(self, name, samesync):
        self.name = name
        self.samesync = samesync
        self.sem = None
        self.count = 0
        self.ops = []
        self.seen = {}


class KB:
    EPOCH = 12000
    NDMA = 12

    def __init__(self, nc, stack):
        self.nc = nc
        self.stack = stack
        self.tstack = stack
        self.pfx = ""
        self.engs = {
            "pe": _Eng("pe", False),
            "act": _Eng("act", True),
            "dve": _Eng("dve", True),
            "pool": _Eng("pool", True),
            "sp": _Eng("sp", False),
        }
        self.nsem = 0
        for e in self.engs.values():
            e.sem = self._newsem(e.name)
        self.dma_sems = {}
        for q in ("sp", "pool", "act"):
            self.dma_sems[q] = [[self._newsem(f"d{q}{i}"), 0] for i in range(self.NDMA)]
        self.dma_rr = {"sp": 0, "pool": 0, "act": 0}
        self.out_events = []
        self.nbuf = 0

    def _newsem(self, name):
        self.nsem += 1
        return self.stack.enter_context(self.nc.semaphore(f"s_{name}_{self.nsem}"))

    def buf(self, name=None):
        self.nbuf += 1
        return Buf(name or f"b{self.nbuf}")

    def bufs(self, n, name="b"):
        return [self.buf(f"{name}{i}") for i in range(n)]

    def sbuf(self, name, shape, dtype):
        return self.tstack.enter_context(self.nc.sbuf_tensor("sb_" + self.pfx + name, list(shape), dtype))

    def psum(self, name, shape, dtype=F32):
        return self.tstack.enter_context(self.nc.psum_tensor("pp_" + self.pfx + name, list(shape), dtype))

    def _collect(self, eng, reads, writes):
        need = {}

        def add(ev):
            sem, val, en = ev
            if en == eng.name and not eng.samesync:
                return
            k = id(sem)
            if eng.seen.get(k, 0) >= val:
                return
            if k not in need or need[k][1] < val:
                need[k] = (sem, val)

        for b in reads:
            if b.writer is not None:
                add(b.writer)
            if b.excl:
                for r in b.readers:
                    if r[2] != eng.name:
                        add(r)
        for b in writes:
            if b.writer is not None:
                add(b.writer)
            for r in b.readers:
                add(r)
        waits = list(need.values())
        for sem, val in waits:
            eng.seen[id(sem)] = val
        return waits

    def _commit(self, ev, reads, writes):
        for b in reads:
            b.readers.append(ev)
            if len(b.readers) > 64:
                last = {}
                for r in b.readers:
                    k = id(r[0])
                    if k not in last or last[k][1] < r[1]:
                        last[k] = r
                b.readers = list(last.values())
        for b in writes:
            b.writer = ev
            b.readers = []

    def op(self, engname, fn, reads=(), writes=()):
        eng = self.engs[engname]
        waits = self._collect(eng, reads, writes)
        if eng.count >= self.EPOCH:
            eng.sem = self._newsem(eng.name)
            eng.count = 0
        eng.count += 1
        ev = (eng.sem, eng.count, eng.name)
        eng.ops.append((waits, fn, (eng.sem, 1)))
        self._commit(ev, reads, writes)
        return ev

    def dma(self, queue, out, in_, reads=(), writes=(), is_output=False, **kw):
        eng = self.engs[queue]
        slots = self.dma_sems[queue]
        i = self.dma_rr[queue]
        self.dma_rr[queue] = (i + 1) % len(slots)
        slot = slots[i]
        sem = slot[0]
        waits = self._collect(eng, reads, writes)
        if slot[1] > 0 and eng.seen.get(id(sem), 0) < slot[1]:
            waits.append((sem, slot[1]))
            eng.seen[id(sem)] = slot[1]
        slot[1] += 16
        ev = (sem, slot[1], "dma_" + queue)
        eng.ops.append((waits, (lambda e, o=out, a=in_, k=kw: e.dma_start(out=o, in_=a, **k)), (sem, 16)))
        self._commit(ev, reads, writes)
        if is_output:
            self.out_events.append(ev)
        return ev

    def allgather(self, out, in_, groups, reads=(), writes=()):
        eng = self.engs["pool"]
        sem = self._newsem("cc")
        waits = self._collect(eng, reads, writes)
        ev = (sem, 1, "cc")
        eng.ops.append((waits, (lambda e, o=out, a=in_, g=groups: e.collective_compute(
            "AllGather", ALU.bypass, replica_groups=g, ins=[a], outs=[o])), (sem, 1)))
        self._commit(ev, reads, writes)
        return ev

    def barrier(self):
        evs = []
        for e in self.engs.values():
            if e.count > 0:
                evs.append((e.sem, e.count, e.name))
        for q, slots in self.dma_sems.items():
            for sem, val in slots:
                if val > 0:
                    evs.append((sem, val, "dma_" + q))
        tok = Buf("barrier")
        for name, e in self.engs.items():
            waits = []
            for sem, val, en in evs:
                if en == name:
                    continue
                if e.seen.get(id(sem), 0) >= val:
                    continue
                waits.append((sem, val))
                e.seen[id(sem)] = val
            if waits:
                if e.count >= self.EPOCH:
                    e.sem = self._newsem(e.name)
                    e.count = 0
                e.count += 1
                e.ops.append((waits, (lambda en_: en_.engine_nop()) if name in ("dve", "pool") else None, (e.sem, 1)))

    def flush(self, final=False):
        final_waits = []
        if final:
            last = {}
            for sem, val, _ in self.out_events:
                k = id(sem)
                if k not in last or last[k][1] < val:
                    last[k] = (sem, val)
            final_waits = list(last.values())
        nc = self.nc
        engs = self.engs
        self.nflush = getattr(self, "nflush", 0) + 1
        with nc.named_scope(f"{self.pfx}f{self.nflush}"), nc.Block() as block:
            def replay(e, st, extra=None):
                for waits, fn, inc in st.ops:
                    for sem, val in waits:
                        e.wait_ge(sem, val)
                    if fn is None:
                        e.sem_inc(inc[0], inc[1])
                        continue
                    ins = fn(e)
                    ins.then_inc(inc[0], inc[1])
                if extra:
                    for sem, val in extra:
                        e.wait_ge(sem, val)

            @block.tensor
            def _(e):
                replay(e, engs["pe"])

            @block.scalar
            def _(e):
                replay(e, engs["act"])

            @block.vector
            def _(e):
                replay(e, engs["dve"])

            @block.gpsimd
            def _(e):
                replay(e, engs["pool"])

            @block.sync
            def _(e):
                replay(e, engs["sp"], final_waits)
        for e in engs.values():
            e.ops = []

    def finish(self):
        self.flush(final=True)


def MM(kb, out, lhsT, rhs, start, stop, reads, writes):
    return kb.op("pe", lambda e: e.matmul(out, lhsT=lhsT, rhs=rhs, start=start, stop=stop), reads, writes)


def TR(kb, out, in_, ident, reads, writes):
    return kb.op("pe", lambda e: e.transpose(out, in_, ident), reads, writes)


def ACT(kb, out, in_, func, reads, writes, **kw):
    return kb.op("act", lambda e: e.activation(out, in_, func, **kw), reads, writes)


def TT(kb, eng, out, in0, in1, op, reads, writes):
    return kb.op(eng, lambda e: e.tensor_tensor(out, in0, in1, op), reads, writes)


def TS(kb, eng, out, in0, s1, s2, op0, op1, reads, writes):
    if op1 is None:
        return kb.op(eng, lambda e: e.tensor_scalar(out, in0, s1, None, op0=op0), reads, writes)
    return kb.op(eng, lambda e: e.tensor_scalar(out, in0, s1, s2, op0=op0, op1=op1), reads, writes)


def STT(kb, eng, out, in0, scalar, in1, op0, op1, reads, writes):
    return kb.op(eng, lambda e: e.scalar_tensor_tensor(out, in0, scalar, in1, op0=op0, op1=op1), reads, writes)


def CP(kb, eng, out, in_, reads, writes):
    if eng == "act":
        return kb.op("act", lambda e: e.copy(out, in_), reads, writes)
    return kb.op(eng, lambda e: e.tensor_copy(out, in_), reads, writes)


def RECIP(kb, out, in_, reads, writes):
    return kb.op("dve", lambda e: e.reciprocal(out, in_), reads, writes)


def MEMSET(kb, eng, ap, val, writes):
    return kb.op(eng, lambda e: e.memset(ap, val), (), writes)


class PsumRing:
    def __init__(self, kb, n=8, name="ps"):
        self.t = [kb.psum(f"{name}{i}", [128, 512], F32) for i in range(n)]
        self.b = kb.bufs(n, name)
        for b in self.b:
            b.excl = True
        self.i = 0
        self.n = n
        self.held = [False] * n

    def _pick(self):
        for _ in range(self.n):
            i = self.i
            self.i = (i + 1) % self.n
            if not self.held[i]:
                return i
        raise RuntimeError("all PSUM banks held")

    def next(self):
        i = self._pick()
        return self.t[i], self.b[i]

    def acquire(self):
        i = self._pick()
        self.held[i] = True
        return self.t[i], self.b[i], i

    def release(self, i):
        self.held[i] = False


def rms_stats(kb, ring, ones, b_ones, sq_t, b_sq, srcs, width, rstd_ap, b_rstd, inv_n, tmp_ap=None):
    ps, bps = ring.next()
    n = len(srcs)
    for i, (ap, b) in enumerate(srcs):
        j = i % len(sq_t)
        ACT(kb, sq_t[j][:, 0:width], ap, AF.Square, [b], [b_sq[j]])
        MM(kb, ps[:, 0:width], ones, sq_t[j][:, 0:width], i == 0, i == n - 1, [b_ones, b_sq[j]], [bps])
    TS(kb, "dve", rstd_ap, ps[:, 0:width], inv_n, EPS, ALU.mult, ALU.add, [bps], [b_rstd])
    RECIP(kb, rstd_ap, rstd_ap, [b_rstd], [b_rstd])
    ACT(kb, rstd_ap, rstd_ap, AF.Sqrt, [b_rstd], [b_rstd])


NT = 2048
HAL = 16
TTOK = 512
POOL_WINDOWS = (2, 4, 8, 16)


NT = 2048
HAL = 16
TTOK = 512
POOL_WINDOWS = (2, 4, 8, 16)


class NS:
    pass


def emit_tok(nc, kb, D, do_outproj, do_pool, do_final):
    W = NT + HAL
    with contextlib.ExitStack() as st_:
        kb.tstack = st_
        kb.pfx = D.pfx
        ring = PsumRing(kb, 8)
        xres = kb.sbuf("xres", [128, 8, W], F32)
        bx = [[kb.buf(f"x{kc}_{t}") for t in range(5)] for kc in range(8)]
        ones = kb.sbuf("ones", [128, 128], BF16); b_ones = kb.buf("ones")
        MEMSET(kb, "pool", ones[:], 1.0, [b_ones])
        sq_t = [kb.sbuf(f"sq{i}", [128, TTOK], BF16) for i in range(2)]; b_sq = kb.bufs(2, "sq")
        rstd = kb.sbuf("rstd", [128, TTOK], F32); b_rstd = kb.buf("rstd")
        Wo = [kb.sbuf(f"Wo{i}", [128, 4, 1024], BF16) for i in range(2)]; b_Wo = kb.bufs(2, "Wo")
        ybuf = [kb.sbuf(f"ybuf{i}", [128, 4, TTOK], BF16) for i in range(2)]; b_y = kb.bufs(2, "ybuf")
        flg = kb.sbuf("flg", [128, 2], F32); b_flg = kb.buf("flg")
        kb.dma("sp", flg[:], D.flag_d, writes=[b_flg])

        def colrange(t):
            return (0, HAL) if t == 0 else (HAL + (t - 1) * TTOK, HAL + t * TTOK)

        for t in range(1, 5):
            xm = D.x_main[t - 1].rearrange("(kc p) t -> p kc t", p=128)
            for kc in range(8):
                kb.dma("sp", xres[:, kc, HAL + (t - 1) * TTOK:HAL + t * TTOK], xm[:, kc, :], reads=D.x_reads, writes=[bx[kc][t]])
        if D.x_halo is not None:
            xh = D.x_halo.rearrange("(kc p) t -> p kc t", p=128)
            for kc in range(8):
                kb.dma("sp", xres[:, kc, 0:HAL], xh[:, kc, :], reads=D.xh_reads, writes=[bx[kc][0]])
                if D.halo_flag:
                    TS(kb, "dve", xres[:, kc, 0:HAL], xres[:, kc, 0:HAL], flg[:, 0:1], None, ALU.mult, None,
                       [bx[kc][0], b_flg], [bx[kc][0]])
        else:
            for kc in range(8):
                MEMSET(kb, "pool", xres[:, kc, 0:HAL], 0.0, [bx[kc][0]])

        def outproj_stage(Wo_t, b_w, y_t, b_yy, t, nkc=4):
            c0, c1 = colrange(t)
            for oc in range(8):
                ps, bps = ring.next()
                for kc in range(nkc):
                    MM(kb, ps[:, 0:c1 - c0], Wo_t[:, kc, oc * 128:(oc + 1) * 128], y_t[:, kc, 0:c1 - c0],
                       kc == 0, kc == nkc - 1, [b_w, b_yy], [bps])
                TT(kb, "dve", xres[:, oc, c0:c1], xres[:, oc, c0:c1], ps[:, 0:c1 - c0], ALU.add,
                   [bps, bx[oc][t]], [bx[oc][t]])

        ycnt = 0
        if do_outproj:
            with contextlib.ExitStack() as st1:
                kb.tstack = st1
                ya = [kb.sbuf(f"ya{i}", [128, 4, TTOK], BF16) for i in range(2)]; b_ya = kb.bufs(2, "ya")
                yb_ = [kb.sbuf(f"yb{i}", [128, 4, TTOK], BF16) for i in range(2)]; b_yb = kb.bufs(2, "yb")
                wpv = D.wprev_d.rearrange("(g kc p) n -> g p kc n", p=128, kc=4)
                y0v = [a.rearrange("(g kc p) t -> g p kc t", p=128, kc=4) for a in D.y0]
                y1v = [a.rearrange("(g kc p) t -> g p kc t", p=128, kc=4) for a in D.y1]
                yhv = D.y_halo.rearrange("(g kc p) t -> g p kc t", p=128, kc=4) if D.y_halo is not None else None
                for gi in range(4):
                    kb.dma("pool", Wo[gi % 2][:], wpv[gi], writes=[b_Wo[gi % 2]])
                    for t in range(0 if do_pool else 1, 5):
                        yb = ycnt % 2
                        ycnt += 1
                        if t == 0:
                            kb.dma("sp", ya[yb][:, :, 0:HAL], yhv[gi], reads=D.y_reads, writes=[b_ya[yb]])
                            TS(kb, "dve", ybuf[yb][:, :, 0:HAL], ya[yb][:, :, 0:HAL], flg[:, 0:1], None, ALU.mult, None,
                               [b_ya[yb], b_flg], [b_y[yb]])
                        else:
                            kb.dma("sp", ya[yb][:], y0v[t - 1][gi], reads=D.y_reads, writes=[b_ya[yb]])
                            kb.dma("sp", yb_[yb][:], y1v[t - 1][gi], reads=D.y_reads, writes=[b_yb[yb]])
                            TS(kb, "dve", ya[yb][:], ya[yb][:], flg[:, 1:2], None, ALU.mult, None, [b_ya[yb], b_flg], [b_ya[yb]])
                            STT(kb, "dve", ybuf[yb][:], yb_[yb][:], flg[:, 0:1], ya[yb][:], ALU.mult, ALU.add,
                                [b_yb[yb], b_ya[yb], b_flg], [b_y[yb]])
                        outproj_stage(Wo[gi % 2], b_Wo[gi % 2], ybuf[yb], b_y[yb], t)
                kb.barrier()
                kb.flush()
            kb.tstack = st_

        with contextlib.ExitStack() as st2:
            kb.tstack = st2
            if do_pool:
                g_d, win_d, wgrp_d, wout_d, scale_d, invc_d = D.g_d, D.win_d, D.wgrp_d, D.wout_d, D.scale_d, D.invc_d
                gsb = kb.sbuf("gsb", [128, 8], F32); b_g = kb.buf("g")
                scl = kb.sbuf("scl", [128, 16], F32); b_scl = kb.buf("scl")
                invc = kb.sbuf("invc", [128, 64], F32); b_invc = kb.buf("invc")
                kb.dma("sp", gsb[:], g_d, writes=[b_g])
                kb.dma("sp", scl[:], scale_d, writes=[b_scl])
                kb.dma("sp", invc[:], invc_d, writes=[b_invc])
                hT = kb.sbuf("hT", [128, 8, W], BF16)
                bh = [kb.buf(f"h{t}") for t in range(5)]
                W1 = [kb.sbuf(f"W1_{i}", [128, 8, 1024], BF16) for i in range(2)]; b_W1 = kb.bufs(2, "W1")
                Wg = [kb.sbuf(f"Wg{i}", [128, 4, 512], BF16) for i in range(2)]; b_Wg = kb.bufs(2, "Wg")
                ubuf = [kb.sbuf(f"ubuf{i}", [128, 4, TTOK + HAL], F32) for i in range(2)]
                b_u = [[kb.buf(f"u{i}_{cc}") for cc in range(4)] for i in range(2)]
                b_uc = [[kb.buf(f"uc{i}_{cc}") for cc in range(4)] for i in range(2)]
                sg = [kb.sbuf(f"sg{i}", [128, 4, TTOK], BF16) for i in range(2)]; b_sg = [kb.bufs(4, f"sg{i}_") for i in range(2)]
                tmp = [kb.sbuf(f"ptmp{i}", [128, TTOK + HAL], F32) for i in range(2)]; b_tmp = kb.bufs(2, "ptmp")
                t16 = kb.sbuf("t16", [128, HAL], F32); b_t16 = kb.buf("t16")
                pbuf = [kb.sbuf(f"pbuf{i}", [128, 4, TTOK], BF16) for i in range(2)]; b_p = kb.bufs(2, "pbuf")

                for t in range(5):
                    c0, c1 = colrange(t)
                    wd = c1 - c0
                    rms_stats(kb, ring, ones[:], b_ones, sq_t, b_sq,
                              [(xres[:, kc, c0:c1], bx[kc][t]) for kc in range(8)], wd, rstd[:, 0:wd], b_rstd, 1.0 / 1024)
                    for kc in range(8):
                        STT(kb, "dve", hT[:, kc, c0:c1], xres[:, kc, c0:c1], gsb[:, kc:kc + 1], rstd[:, 0:wd],
                            ALU.mult, ALU.mult, [bx[kc][t], b_g, b_rstd], [bh[t]])

                winv = win_d.rearrange("(kc p) n -> p kc n", p=128)
                wgv = wgrp_d.rearrange("g (kc p) n -> g p kc n", p=128)
                wov = wout_d.rearrange("(g kc p) n -> g p kc n", p=128, kc=4)

                def load_group(gi):
                    s = gi % 2
                    for kc in range(8):
                        kb.dma("pool", W1[s][:, kc, 0:512], winv[:, kc, gi * 512:(gi + 1) * 512], writes=[b_W1[s]])
                        kb.dma("pool", W1[s][:, kc, 512:1024], winv[:, kc, 2048 + gi * 512:2048 + (gi + 1) * 512],
                               writes=[b_W1[s]])
                    kb.dma("pool", Wg[s][:], wgv[gi], writes=[b_Wg[s]])
                    kb.dma("pool", Wo[s][:], wov[gi], writes=[b_Wo[s]])

                load_group(0)
                load_group(1)

                def ub_of(gi, t):
                    return (gi * 4 + t - 1) % 2

                def emit_halo(gi):
                    s = gi % 2
                    ub = ub_of(gi, 1)
                    for cc in range(4):
                        ps, bps = ring.next()
                        for kc in range(8):
                            MM(kb, ps[:, 0:HAL], W1[s][:, kc, cc * 128:(cc + 1) * 128], hT[:, kc, 0:HAL],
                               kc == 0, kc == 7, [b_W1[s], bh[0]], [bps])
                        CP(kb, "act", ubuf[ub][:, cc, 0:HAL], ps[:, 0:HAL], [bps], [b_uc[ub][cc]])

                def emit_inproj(gi, t):
                    s = gi % 2
                    c0, c1 = colrange(t)
                    ub = ub_of(gi, t)
                    sgi = t % 2
                    for cc in range(4):
                        ps, bps = ring.next()
                        for kc in range(8):
                            MM(kb, ps[:], W1[s][:, kc, cc * 128:(cc + 1) * 128], hT[:, kc, c0:c1],
                               kc == 0, kc == 7, [b_W1[s], bh[t]], [bps])
                        CP(kb, "act", ubuf[ub][:, cc, HAL:HAL + TTOK], ps[:], [bps], [b_u[ub][cc]])
                    for cc in range(4):
                        ps, bps = ring.next()
                        for kc in range(8):
                            MM(kb, ps[:], W1[s][:, kc, 512 + cc * 128:512 + (cc + 1) * 128], hT[:, kc, c0:c1],
                               kc == 0, kc == 7, [b_W1[s], bh[t]], [bps])
                        ACT(kb, sg[sgi][:, cc, :], ps[:], AF.Silu, [bps], [b_sg[sgi][cc]])

                def emit_rest(gi, t):
                    nonlocal ycnt
                    s = gi % 2
                    win = POOL_WINDOWS[gi]
                    ub = ub_of(gi, t)
                    nub = 1 - ub
                    sgi = t % 2
                    pb = t % 2
                    for cc in range(4):
                        U = ubuf[ub][:, cc, :]
                        ru = [b_u[ub][cc], b_uc[ub][cc]]
                        ta, tb = tmp[0], tmp[1]
                        bta, btb = b_tmp[0], b_tmp[1]
                        Wd = TTOK + HAL
                        TT(kb, "dve", ta[:, 1:Wd], U[:, 1:Wd], U[:, 0:Wd - 1], ALU.add, ru, [bta])
                        cur, bcur, oth, both = ta, bta, tb, btb
                        sh = 2
                        lo = 1
                        while sh < win:
                            lo2 = lo + sh
                            TT(kb, "dve", oth[:, lo2:Wd], cur[:, lo2:Wd], cur[:, lo2 - sh:Wd - sh], ALU.add, [bcur], [both])
                            cur, bcur, oth, both = oth, both, cur, bcur
                            lo = lo2
                            sh *= 2
                        STT(kb, "dve", pbuf[pb][:, cc, :], cur[:, HAL:Wd], 1.0 / win, U[:, HAL:Wd],
                            ALU.mult, ALU.subtract, [bcur] + ru, [b_p[pb]])
                        if t == 1:
                            TT(kb, "dve", t16[:], cur[:, HAL:2 * HAL], invc[:, gi * 16:(gi + 1) * 16], ALU.mult,
                               [bcur, b_invc], [b_t16])
                            TT(kb, "dve", pbuf[pb][:, cc, 0:HAL], t16[:], U[:, HAL:2 * HAL], ALU.subtract,
                               [b_t16] + ru, [b_p[pb]])
                        if t < 4:
                            CP(kb, "pool", ubuf[nub][:, cc, 0:HAL], U[:, TTOK:TTOK + HAL], ru, [b_uc[nub][cc]])
                    yb = ycnt % 2
                    ycnt += 1
                    for oc in range(4):
                        ps, bps = ring.next()
                        for kc in range(4):
                            MM(kb, ps[:], Wg[s][:, kc, oc * 128:(oc + 1) * 128], pbuf[pb][:, kc, :],
                               kc == 0, kc == 3, [b_Wg[s], b_p[pb]], [bps])
                        STT(kb, "dve", ybuf[yb][:, oc, :], ps[:], scl[:, gi * 4 + oc:gi * 4 + oc + 1], sg[sgi][:, oc, :],
                            ALU.mult, ALU.mult, [bps, b_scl, b_sg[sgi][oc]], [b_y[yb]])
                    outproj_stage(Wo[s], b_Wo[s], ybuf[yb], b_y[yb], t)

                seq_ = [(gi, t) for gi in range(4) for t in range(1, 5)]
                emit_halo(0)
                emit_inproj(0, 1)
                for i_, (gi, t) in enumerate(seq_):
                    if i_ + 1 < len(seq_):
                        ngi, nt = seq_[i_ + 1]
                        if nt == 1:
                            emit_halo(ngi)
                        emit_inproj(ngi, nt)
                    emit_rest(gi, t)
                    if t == 4 and gi + 2 < 4:
                        load_group(gi + 2)
            if do_final:
                fgs = kb.sbuf("fgs", [128, 8], F32); b_fg = kb.buf("fg")
                kb.dma("sp", fgs[:], D.fg_d, writes=[b_fg])
                for t in range(1, 5):
                    c0, c1 = colrange(t)
                    rms_stats(kb, ring, ones[:], b_ones, sq_t, b_sq,
                              [(xres[:, kc, c0:c1], bx[kc][t]) for kc in range(8)], TTOK, rstd[:], b_rstd, 1.0 / 1024)
                    for kc in range(8):
                        STT(kb, "dve", xres[:, kc, c0:c1], xres[:, kc, c0:c1], fgs[:, kc:kc + 1], rstd[:],
                            ALU.mult, ALU.mult, [bx[kc][t], b_fg, b_rstd], [bx[kc][t]])
            for t in range(1, 5):
                c0, c1 = colrange(t)
                ov = D.out[t - 1].rearrange("(kc p) t -> p kc t", p=128)
                for kc in range(8):
                    kb.dma("sp", ov[:, kc, :], xres[:, kc, c0:c1], reads=[bx[kc][t]],
                           writes=([] if D.out_is_output else [D.b_out[t - 1]]), is_output=D.out_is_output)
                if D.after_tile is not None:
                    D.after_tile(t - 1)
            kb.barrier()
            kb.flush()
    kb.tstack = kb.stack


def _colmajor128(v, n):
    return np.ascontiguousarray(np.asarray(v, np.float32).reshape(n, 128).T)


def _halo_T(x_b, r):
    F = x_b.shape[1]
    out = np.zeros((F, HAL + NT), x_b.dtype)
    lo = r * NT
    if r > 0:
        out[:, :HAL] = x_b[lo - HAL:lo].T
    out[:, HAL:] = x_b[lo:lo + NT].T
    return out


def _invcnt(r):
    t = np.arange(16)
    tab = np.zeros((128, 64), np.float32)
    for gi, w in enumerate(POOL_WINDOWS):
        cnt = np.minimum(t + 1, w) if r == 0 else np.full(16, w)
        tab[:, gi * 16:(gi + 1) * 16] = (1.0 / cnt.astype(np.float32))[None, :]
    return tab


_NC_CACHE = {}


def _get_nc(key, builder):
    if key not in _NC_CACHE:
        _NC_CACHE[key] = builder()
    return _NC_CACHE[key]


SEQ = 4096
GH = 4
NEG = -30000.0


SEQ = 4096
GH = 4
MH = 8
NEG = -30000.0
C1_2PI = 6.28125
C2_2PI = 2.0 * np.pi - 6.28125


def emit_gdn(nc, kb, D, ntile=8, dbg=9):
    S = ntile * TTOK
    with contextlib.ExitStack() as st_:
        kb.tstack = st_
        kb.pfx = D.pfx
        ring = PsumRing(kb, 8)
        ones = kb.sbuf("ones", [128, 128], F32); b_ones = kb.buf("ones")
        MEMSET(kb, "pool", ones[:], 1.0, [b_ones])
        onesb = kb.sbuf("onesb", [128, 128], BF16); b_onesb = kb.buf("onesb")
        MEMSET(kb, "pool", onesb[:], 1.0, [b_onesb])
        cst = kb.sbuf("cst", [128, 5 * 128], F32); b_cst = kb.buf("cst")
        kb.dma("sp", cst[:], D.cst_d, writes=[b_cst])
        ident, TRI, MASKI, MASKS, MASKP = [cst[:, i * 128:(i + 1) * 128] for i in range(5)]
        gsb = kb.sbuf("gsb", [128, 8], F32); b_g = kb.buf("g")
        kb.dma("sp", gsb[:], D.g_d, writes=[b_g])
        cw = kb.sbuf("cw", [128, 64], F32); b_cw = kb.buf("cw")
        kb.dma("sp", cw[:], D.cw_d, writes=[b_cw])
        hp = kb.sbuf("hp", [128, 8], F32); b_hp = kb.buf("hp")
        kb.dma("sp", hp[:], D.hp_d, writes=[b_hp])
        ngb = kb.sbuf("ngb", [128, 256], F32); b_ng = kb.buf("ng")
        kb.dma("sp", ngb[:], D.ng_d, writes=[b_ng])
        nA = kb.sbuf("nA", [128, 4], F32); b_nA = kb.buf("nA")
        ACT(kb, nA[:], hp[:, 0:4], AF.Exp, [b_hp], [b_nA])
        TS(kb, "dve", nA[:], nA[:], -1.0, None, ALU.mult, None, [b_nA], [b_nA])

        Wf = kb.sbuf("Wf", [128, 8, GH * 512], BF16); b_Wf = kb.bufs(8, "Wf")
        Wg = kb.sbuf("Wg", [128, 8, GH * 256], BF16); b_Wg = kb.bufs(8, "Wg")
        Wba = kb.sbuf("Wba", [128, 8, 8], BF16); b_Wba = kb.buf("Wba")
        wfv = D.wf_d.rearrange("(kc p) n -> p kc n", p=128)
        wgv = D.wg_d.rearrange("(kc p) n -> p kc n", p=128)
        for kc in range(8):
            kb.dma("pool", Wf[:, kc, :], wfv[:, kc, :], writes=[b_Wf[kc]])
        for kc in range(8):
            kb.dma("pool", Wg[:, kc, :], wgv[:, kc, :], writes=[b_Wg[kc]])
        kb.dma("pool", Wba[:], D.wba_d.rearrange("(kc p) n -> p kc n", p=128), writes=[b_Wba])

        xt = kb.sbuf("xt", [128, 8, TTOK], F32); b_xt = kb.bufs(8, "xt")
        hT = kb.sbuf("hT", [128, 8, TTOK], BF16); b_h = kb.buf("hT")
        sq_t = [kb.sbuf(f"sq{i}", [128, TTOK], BF16) for i in range(2)]; b_sq = kb.bufs(2, "sq")
        rstd = kb.sbuf("rstd", [128, TTOK], F32); b_rstd = kb.buf("rstd")
        raw = [kb.sbuf(f"raw{i}", [128, TTOK + 3], F32) for i in range(4)]; b_raw = kb.bufs(4, "raw")
        carry = kb.sbuf("carry", [128, 16, 3], F32); b_carry = kb.bufs(16, "carry")
        MEMSET(kb, "pool", carry[:], 0.0, b_carry)
        cv = [kb.sbuf(f"cv{i}", [128, TTOK], F32) for i in range(2)]; b_cv = kb.bufs(2, "cv")
        qk32 = [kb.sbuf(f"qk32_{i}", [128, TTOK], F32) for i in range(2)]; b_qk32 = kb.bufs(2, "qk32")
        kn32 = kb.sbuf("kn32", [128, TTOK], F32); b_kn32 = kb.buf("kn32")
        rs = kb.sbuf("rs", [128, TTOK], F32); b_rs = kb.buf("rs")
        KQ = [kb.sbuf(f"KQ{h}", [128, 4, 2, 128], BF16) for h in range(GH)]
        b_KQ = [kb.bufs(2, f"KQ{h}_") for h in range(GH)]
        sgate = [kb.sbuf(f"sgate{c}", [128, GH * 256], BF16) for c in range(4)]; b_sgate = kb.bufs(4, "sgate")
        bsig = [kb.sbuf(f"bsig{c}", [128, 4], F32) for c in range(4)]
        lnb = [kb.sbuf(f"lnb{c}", [128, 4], F32) for c in range(4)]
        gcol = [kb.sbuf(f"gcol{c}", [128, 4], F32) for c in range(4)]
        gc = [kb.sbuf(f"gc{c}", [128, 4], F32) for c in range(4)]
        ngc = [kb.sbuf(f"ngc{c}", [128, 4], F32) for c in range(4)]
        gl = [kb.sbuf(f"gl{c}", [128, 4], F32) for c in range(4)]
        beg = [kb.sbuf(f"beg{c}", [128, 4], F32) for c in range(4)]
        kds = [kb.sbuf(f"kds{c}", [128, 4], F32) for c in range(4)]
        glast = [kb.sbuf(f"glast{c}", [128, 4], F32) for c in range(4)]
        b_tok = kb.bufs(4, "tokscal")
        tmp8 = kb.sbuf("tmp8", [128, 8], F32); b_tmp8 = kb.buf("tmp8")
        spt = kb.sbuf("spt", [128, 16], F32)
        Kbg = [[kb.sbuf(f"Kbg{h}_{c}", [128, 128], BF16) for c in range(4)] for h in range(GH)]
        Kdec = [[kb.sbuf(f"Kdec{h}_{c}", [128, 128], BF16) for c in range(4)] for h in range(GH)]
        Vb = [[kb.sbuf(f"Vb{h}_{c}", [128, 256], BF16) for c in range(4)] for h in range(GH)]
        b_tm = [[kb.buf(f"tm{h}_{c}") for c in range(4)] for h in range(GH)]
        NSET = 4
        Gb = [kb.sbuf(f"Gb{i}", [128, 128], F32) for i in range(NSET)]; b_Gb = kb.bufs(NSET, "Gb")
        LNBb = [kb.sbuf(f"LNBb{i}", [128, 128], F32) for i in range(NSET)]; b_LNBb = kb.bufs(NSET, "LNBb")
        EG = [kb.sbuf(f"EG{i}", [128, 128], F32) for i in range(NSET)]; b_EG = kb.bufs(NSET, "EG")
        EGl = kb.sbuf("EGl", [128, 16], F32)
        DT = [kb.sbuf(f"DT{i}", [128, 3, 128], F32) for i in range(NSET)]; b_DT = kb.bufs(NSET, "DT")
        XA = [[kb.sbuf(f"XA{i}_{j}", [128, 128], F32) for j in range(2)] for i in range(NSET)]
        XTA = [[kb.sbuf(f"XTA{i}_{j}", [128, 128], F32) for j in range(2)] for i in range(NSET)]
        RA = [[kb.sbuf(f"RA{i}_{j}", [128, 128], F32) for j in range(2)] for i in range(NSET)]
        b_XA = [kb.bufs(2, f"XA{i}_") for i in range(NSET)]
        b_XTA = [kb.bufs(2, f"XTA{i}_") for i in range(NSET)]
        b_RA = [kb.bufs(2, f"RA{i}_") for i in range(NSET)]
        Rb = [kb.sbuf(f"Rb{i}", [128, 128], BF16) for i in range(NSET)]; b_Rb = kb.bufs(NSET, "Rb")
        AT = [[kb.sbuf(f"AT{h}_{c}", [128, 128], BF16) for c in range(2)] for h in range(GH)]
        U = [[kb.sbuf(f"U{h}_{c}", [128, 256], F32) for c in range(2)] for h in range(GH)]
        nWT = [[kb.sbuf(f"nWT{h}_{c}", [128, 128], F32) for c in range(2)] for h in range(GH)]
        QgT = [[kb.sbuf(f"QgT{h}_{c}", [128, 128], F32) for c in range(2)] for h in range(GH)]
        b_pre = [[kb.buf(f"pre{h}_{c}") for c in range(2)] for h in range(GH)]
        Vn = [kb.sbuf(f"Vn{h}", [128, 256], BF16) for h in range(GH)]; b_Vn = kb.bufs(GH, "Vn")
        Sst = [kb.sbuf(f"S{h}", [128, 256], F32) for h in range(GH)]; b_S = kb.bufs(GH, "S")
        for h in range(GH):
            MEMSET(kb, "pool", Sst[h][:], 0.0, [b_S[h]])
        osq = [kb.sbuf(f"osq{h}", [128, 256], F32) for h in range(GH)]; b_osq = kb.bufs(GH, "osq")
        oss = [kb.sbuf(f"oss{h}", [128, 1], F32) for h in range(GH)]; b_oss = kb.bufs(GH, "oss")
        o32 = [kb.sbuf(f"o32_{h}", [128, 256], F32) for h in range(GH)]; b_o32 = kb.bufs(GH, "o32")
        ogt = [kb.sbuf(f"ogt{i}", [128, GH * 256], F32) for i in range(2)]; b_ogt = kb.bufs(2, "ogt")
        ogT = [kb.sbuf(f"ogT{i}", [128, 8, 128], BF16) for i in range(2)]; b_ogT = kb.bufs(2, "ogT")

        ogcnt = 0
        for T in range(ntile):
            t0 = T * TTOK
            for kc in range(8):
                kb.dma("sp", xt[:, kc, :], D.xtile(T, kc), reads=D.x_reads, writes=[b_xt[kc]])
            rms_stats(kb, ring, onesb[:], b_onesb, sq_t, b_sq, [(xt[:, kc, :], b_xt[kc]) for kc in range(8)],
                      TTOK, rstd[:], b_rstd, 1.0 / 1024)
            for kc in range(8):
                STT(kb, "dve", hT[:, kc, :], xt[:, kc, :], gsb[:, kc:kc + 1], rstd[:], ALU.mult, ALU.mult,
                    [b_xt[kc], b_g, b_rstd], [b_h])
            for c in range(4 if dbg >= 2 else 0):
                cs = slice(c * 128, (c + 1) * 128)
                for half in range(2):
                    ps, bps = ring.next()
                    for kc in range(8):
                        MM(kb, ps[:], hT[:, kc, cs], Wg[:, kc, half * 512:(half + 1) * 512], kc == 0, kc == 7,
                           [b_h, b_Wg[kc]], [bps])
                    ACT(kb, sgate[c][:, half * 512:(half + 1) * 512], ps[:], AF.Silu, [bps], [b_sgate[c]])
                ps, bps = ring.next()
                for kc in range(8):
                    MM(kb, ps[:, 0:8], hT[:, kc, cs], Wba[:, kc, :], kc == 0, kc == 7, [b_h, b_Wba], [bps])
                bt = b_tok[c]
                ACT(kb, bsig[c][:], ps[:, 0:4], AF.Sigmoid, [bps], [bt])
                ACT(kb, lnb[c][:], bsig[c][:], AF.Ln, [bt], [bt])
                TT(kb, "dve", tmp8[:, 0:4], ps[:, 4:8], hp[:, 4:8], ALU.add, [bps, b_hp], [b_tmp8])
                z_ = tmp8[:, 0:4]; y_ = tmp8[:, 4:8]
                w_ = spt[:, 0:4]; w2_ = spt[:, 4:8]; P_ = spt[:, 8:12]; m_ = spt[:, 12:16]
                t8 = [b_tmp8]
                TS(kb, "dve", y_, z_, -1.0, None, ALU.mult, None, t8, t8)
                TT(kb, "dve", y_, y_, z_, ALU.max, t8, t8)
                ACT(kb, y_, y_, AF.Exp, t8, t8, scale=-1.0)
                TS(kb, "dve", w_, y_, 2.0, None, ALU.add, None, t8, t8)
                RECIP(kb, w_, w_, t8, t8)
                TT(kb, "dve", w_, w_, y_, ALU.mult, t8, t8)
                TT(kb, "dve", w2_, w_, w_, ALU.mult, t8, t8)
                TS(kb, "dve", P_, w2_, 1.0 / 15, 1.0 / 13, ALU.mult, ALU.add, t8, t8)
                for cf in (11, 9, 7, 5, 3, 1):
                    TT(kb, "dve", P_, P_, w2_, ALU.mult, t8, t8)
                    TS(kb, "dve", P_, P_, 1.0 / cf, None, ALU.add, None, t8, t8)
                TT(kb, "dve", P_, P_, w_, ALU.mult, t8, t8)
                TS(kb, "dve", m_, z_, 0.0, None, ALU.max, None, t8, t8)
                STT(kb, "dve", P_, P_, 2.0, m_, ALU.mult, ALU.add, t8, t8)
                TT(kb, "dve", gcol[c][:], P_, nA[:], ALU.mult, [b_tmp8, b_nA], [bt])
                ps2, bps2 = ring.next()
                MM(kb, ps2[:, 0:4], TRI, gcol[c][:], True, True, [b_cst, bt], [bps2])
                CP(kb, "dve", gc[c][:], ps2[:, 0:4], [bps2], [bt])
                TS(kb, "dve", ngc[c][:], ps2[:, 0:4], -1.0, None, ALU.mult, None, [bps2], [bt])
                TT(kb, "dve", gl[c][:], gc[c][:], lnb[c][:], ALU.add, [bt], [bt])
                ACT(kb, beg[c][:], gl[c][:], AF.Exp, [bt], [bt])
                ps3, bps3 = ring.next()
                MM(kb, ps3[:, 0:4], ones[:], gcol[c][:], True, True, [b_ones, bt], [bps3])
                CP(kb, "dve", glast[c][:], ps3[:, 0:4], [bps3], [bt])
                TT(kb, "dve", kds[c][:], glast[c][:], gc[c][:], ALU.subtract, [bt], [bt])
                ACT(kb, kds[c][:], kds[c][:], AF.Exp, [bt], [bt])
            for h in range(GH if dbg >= 1 else 0):
                for p in range(4):
                    ci = h * 4 + p
                    col = h * 512 + p * 128
                    ps, bps = ring.next()
                    for kc in range(8):
                        MM(kb, ps[:], Wf[:, kc, col:col + 128], hT[:, kc, :], kc == 0, kc == 7, [b_Wf[kc], b_h], [bps])
                    CP(kb, "pool", raw[p][:, 0:3], carry[:, ci, :], [b_carry[ci]], [b_raw[p]])
                    CP(kb, "act", raw[p][:, 3:3 + TTOK], ps[:], [bps], [b_raw[p]])
                    CP(kb, "pool", carry[:, ci, :], raw[p][:, TTOK:TTOK + 3], [b_raw[p]], [b_carry[ci]])
                    ce = "dve"
                    c_ = cv[p % 2]; bc_ = b_cv[p % 2]
                    TS(kb, ce, c_[:], raw[p][:, 3:3 + TTOK], cw[:, ci * 4 + 3:ci * 4 + 4], None, ALU.mult, None,
                       [b_raw[p], b_cw], [bc_])
                    for k in (2, 1, 0):
                        STT(kb, ce, c_[:], raw[p][:, k:k + TTOK], cw[:, ci * 4 + k:ci * 4 + k + 1], c_[:],
                            ALU.mult, ALU.add, [b_raw[p], b_cw, bc_], [bc_])
                    if p < 2:
                        ACT(kb, qk32[p][:], c_[:], AF.Silu, [bc_], [b_qk32[p]])
                        rms_stats(kb, ring, onesb[:], b_onesb, sq_t, b_sq, [(qk32[p][:], b_qk32[p])], TTOK, rs[:], b_rs, 1.0)
                        if p == 0:
                            STT(kb, "dve", KQ[h][:, :, 1, :], qk32[0][:].rearrange("p (c t) -> p c t", c=4),
                                128.0 ** -0.5, rs[:].rearrange("p (c t) -> p c t", c=4), ALU.mult, ALU.mult,
                                [b_qk32[0], b_rs], [b_KQ[h][1]])
                        else:
                            TT(kb, "dve", kn32[:], qk32[1][:], rs[:], ALU.mult, [b_qk32[1], b_rs], [b_kn32])
                            CP(kb, "pool", KQ[h][:, :, 0, :], kn32[:].rearrange("p (c t) -> p c t", c=4),
                               [b_kn32], [b_KQ[h][0]])
                            pk, bpk = ring.next()
                            for c in range(4):
                                TR(kb, pk[:, c * 128:(c + 1) * 128], kn32[:, c * 128:(c + 1) * 128], ident, [b_kn32, b_cst], [bpk])
                            for c in range(4):
                                TS(kb, "dve", Kbg[h][c][:], pk[:, c * 128:(c + 1) * 128], beg[c][:, h:h + 1], None, ALU.mult, None,
                                   [bpk, b_tok[c]], [b_tm[h][c]])
                                TS(kb, "dve", Kdec[h][c][:], pk[:, c * 128:(c + 1) * 128], kds[c][:, h:h + 1], None, ALU.mult, None,
                                   [bpk, b_tok[c]], [b_tm[h][c]])
                    else:
                        vt_ = qk32[p - 2]; bvt_ = b_qk32[p - 2]
                        ACT(kb, vt_[:], c_[:], AF.Silu, [bc_], [bvt_])
                        pk, bpk = ring.next()
                        for c in range(4):
                            TR(kb, pk[:, c * 128:(c + 1) * 128], vt_[:, c * 128:(c + 1) * 128], ident, [bvt_, b_cst], [bpk])
                        for c in range(4):
                            TS(kb, "dve", Vb[h][c][:, (p - 2) * 128:(p - 1) * 128], pk[:, c * 128:(c + 1) * 128],
                               bsig[c][:, h:h + 1], None, ALU.mult, None, [bpk, b_tok[c]], [b_tm[h][c]])
            def pre_gen(h, c):
                si = h
                bt = b_tok[c]
                TS(kb, "dve", Gb[si][:], ones[:], gcol[c][:, h:h + 1], None, ALU.mult, None, [b_ones, bt], [b_Gb[si]])
                TS(kb, "pool", LNBb[si][:], ones[:], lnb[c][:, h:h + 1], None, ALU.mult, None, [b_ones, bt], [b_LNBb[si]])
                yield
                pb, bpb, ipb = ring.acquire()
                rd = [b_Gb[si], b_cst]
                MM(kb, pb[:, 0:128], Gb[si][:], TRI, True, True, rd, [bpb])
                MM(kb, pb[:, 128:256], Gb[si][:], TRI, True, False, rd, [bpb])
                MM(kb, pb[:, 128:256], ident, MASKI, False, True, rd, [bpb])
                MM(kb, pb[:, 256:384], Gb[si][:], TRI, True, False, rd, [bpb])
                MM(kb, pb[:, 256:384], LNBb[si][:], ident, False, False, rd + [b_LNBb[si]], [bpb])
                MM(kb, pb[:, 256:384], ident, MASKS, False, True, rd, [bpb])
                MM(kb, pb[:, 384:512], Gb[si][:], TRI, True, False, rd, [bpb])
                MM(kb, pb[:, 384:512], ident, MASKP, False, True, rd, [bpb])
                yield
                ACT(kb, EG[si][:], pb[:, 0:128], AF.Exp, [bpb], [b_EG[si]])
                ACT(kb, DT[si][:, 1, :], pb[:, 256:384], AF.Exp, [bpb, bt], [b_DT[si]], bias=ngc[c][:, h:h + 1])
                ACT(kb, DT[si][:, 2, :], pb[:, 384:512], AF.Exp, [bpb, bt], [b_DT[si]], bias=gl[c][:, h:h + 1], scale=-1.0)
                ACT(kb, DT[si][:, 0, :], pb[:, 128:256], AF.Exp, [bpb, bt], [b_DT[si]], bias=ngc[c][:, h:h + 1])
                ring.release(ipb)
                CP(kb, "pool", EGl[:, h * 4 + c:h * 4 + c + 1], EG[si][:, 127:128], [b_EG[si]], [b_pre[h][c % 2]])
                pg, bpg, ipg = ring.acquire()
                MM(kb, pg[:, 0:256], KQ[h][:, c, 0, :], KQ[h][:, c, :, :].rearrange("p a t -> p (a t)"), True, True,
                   b_KQ[h], [bpg])
                yield
                x_, xt_, r_ = XA[si], XTA[si], RA[si]
                bx_, bxt_, br_ = b_XA[si], b_XTA[si], b_RA[si]
                STT(kb, "dve", x_[0][:], pg[:, 0:128], -1.0, DT[si][:, 1, :], ALU.mult, ALU.mult, [bpg, b_DT[si]], [bx_[0]])
                STT(kb, "dve", xt_[0][:], pg[:, 0:128], -1.0, DT[si][:, 2, :], ALU.mult, ALU.mult, [bpg, b_DT[si]], [bxt_[0]])
                TT(kb, "dve", AT[h][c % 2][:], pg[:, 128:256], DT[si][:, 0, :], ALU.mult, [bpg, b_DT[si]], [b_pre[h][c % 2]])
                ring.release(ipg)
                yield
                TT(kb, "pool", r_[0][:], x_[0][:], ident, ALU.add, [bx_[0], b_cst], [br_[0]])
                cur = 0
                pc_, bpc, ipc = ring.acquire()
                MM(kb, pc_[:, 0:128], x_[0][:], xt_[0][:], True, True, [bx_[0], bxt_[0]], [bpc])
                MM(kb, pc_[:, 128:256], xt_[0][:], x_[0][:], True, True, [bx_[0], bxt_[0]], [bpc])
                yield
                for k in range(1, 7):
                    nxt = 1 - cur
                    CP(kb, "dve", xt_[nxt][:], pc_[:, 0:128], [bpc], [bxt_[nxt]])
                    if k < 6:
                        CP(kb, "dve", x_[nxt][:], pc_[:, 128:256], [bpc], [bx_[nxt]])
                    yield
                    MM(kb, pc_[:, 256:384], xt_[nxt][:], r_[cur][:], True, True, [bxt_[nxt], br_[cur]], [bpc])
                    if k < 6:
                        MM(kb, pc_[:, 0:128], x_[nxt][:], xt_[nxt][:], True, True, [bx_[nxt], bxt_[nxt]], [bpc])
                        if k < 5:
                            MM(kb, pc_[:, 128:256], xt_[nxt][:], x_[nxt][:], True, True, [bx_[nxt], bxt_[nxt]], [bpc])
                    yield
                    TT(kb, "dve", r_[nxt][:], pc_[:, 256:384], r_[cur][:], ALU.add, [bpc, br_[cur]], [br_[nxt]])
                    cur = nxt
                ring.release(ipc)
                CP(kb, "act", Rb[si][:], r_[cur][:], [br_[cur]], [b_Rb[si]])
                R_ = Rb[si]; bR_ = b_Rb[si]
                TT(kb, "pool", QgT[h][c % 2][:], KQ[h][:, c, 1, :], EG[si][:], ALU.mult, [b_KQ[h][1], b_EG[si]], [b_pre[h][c % 2]])
                yield
                pu, bpu, ipu = ring.acquire()
                MM(kb, pu[:, 0:256], R_[:], Vb[h][c][:], True, True, [bR_, b_tm[h][c]], [bpu])
                MM(kb, pu[:, 256:384], Kbg[h][c][:], R_[:], True, True, [bR_, b_tm[h][c]], [bpu])
                yield
                CP(kb, "dve", U[h][c % 2][:], pu[:, 0:256], [bpu], [b_pre[h][c % 2]])
                TS(kb, "dve", nWT[h][c % 2][:], pu[:, 256:384], -1.0, None, ALU.mult, None, [bpu], [b_pre[h][c % 2]])
                ring.release(ipu)

            def seq_gen(h, c, ob):
                pw, bpw, ipw = ring.acquire()
                MM(kb, pw[:, 0:256], nWT[h][c % 2][:], Sst[h][:], True, True, [b_pre[h][c % 2], b_S[h]], [bpw])
                yield
                TT(kb, "dve", Vn[h][:], pw[:, 0:256], U[h][c % 2][:], ALU.add, [bpw, b_pre[h][c % 2]], [b_Vn[h]])
                ring.release(ipw)
                yield
                pq, bpq, ipq = ring.acquire()
                MM(kb, pq[:, 0:256], QgT[h][c % 2][:], Sst[h][:], True, False, [b_pre[h][c % 2], b_S[h]], [bpq])
                MM(kb, pq[:, 0:256], AT[h][c % 2][:], Vn[h][:], False, True, [b_pre[h][c % 2], b_Vn[h]], [bpq])
                MM(kb, pq[:, 256:512], Kdec[h][c][:], Vn[h][:], True, True, [b_tm[h][c], b_Vn[h]], [bpq])
                yield
                STT(kb, "dve", Sst[h][:], Sst[h][:], EGl[:, h * 4 + c:h * 4 + c + 1], pq[:, 256:512], ALU.mult, ALU.add,
                    [b_S[h], b_pre[h][c % 2], bpq], [b_S[h]])
                CP(kb, "act", o32[h][:], pq[:, 0:256], [bpq], [b_o32[h]])
                ring.release(ipq)
                yield
                ACT(kb, osq[h][:], o32[h][:], AF.Square, [b_o32[h]], [b_osq[h]])
                yield
                kb.op("dve", lambda e: e.reduce_sum(oss[h][:], osq[h][:], axis=AX.X), [b_osq[h]], [b_oss[h]])
                TS(kb, "dve", oss[h][:], oss[h][:], 1.0 / 256, EPS, ALU.mult, ALU.add, [b_oss[h]], [b_oss[h]])
                RECIP(kb, oss[h][:], oss[h][:], [b_oss[h]], [b_oss[h]])
                yield
                ACT(kb, oss[h][:], oss[h][:], AF.Sqrt, [b_oss[h]], [b_oss[h]])
                yield
                STT(kb, "dve", osq[h][:], o32[h][:], oss[h][:, 0:1], ngb[:], ALU.mult, ALU.mult, [b_o32[h], b_oss[h], b_ng], [b_osq[h]])
                TT(kb, "dve", ogt[ob][:, h * 256:(h + 1) * 256], osq[h][:], sgate[c][:, h * 256:(h + 1) * 256], ALU.mult,
                   [b_osq[h], b_sgate[c]], [b_ogt[ob]])

            def emit_out(c, ob):
                for hb in range(2):
                    pt_, bpt_ = ring.next()
                    for q_ in range(4):
                        TR(kb, pt_[:, q_ * 128:(q_ + 1) * 128], ogt[ob][:, (hb * 4 + q_) * 128:(hb * 4 + q_ + 1) * 128], ident,
                           [b_ogt[ob], b_cst], [bpt_])
                    CP(kb, "act", ogT[ob][:, hb * 4:(hb + 1) * 4, :], pt_[:].rearrange("p (q t) -> p q t", q=4), [bpt_], [b_ogT[ob]])
                kb.dma("sp", D.out_h[T // 4][T % 4].rearrange("(blk p) t -> p blk t", p=128)[:, :, c * 128:(c + 1) * 128], ogT[ob][:],
                       reads=[b_ogT[ob]], writes=[D.b_out[T // 4][T % 4]])

            def run_pool(gens):
                gens = list(gens)
                while gens:
                    for g_ in list(gens):
                        try:
                            next(g_)
                        except StopIteration:
                            gens.remove(g_)

            run_pool([pre_gen(h, 0) for h in range(GH)])
            prev_out = None
            for c in range(4):
                ob = ogcnt % 2
                ogcnt += 1
                pool_ = [seq_gen(h, c, ob) for h in range(GH)]
                if c < 3:
                    pool_ += [pre_gen(h, c + 1) for h in range(GH)]
                run_pool(pool_)
                emit_out(c, ob)
            if D.after_tile is not None:
                D.after_tile(T)
        kb.barrier()
        kb.flush()
    kb.tstack = kb.stack


def _gdn_consts():
    j = np.arange(128)[:, None]
    i = np.arange(128)[None, :]
    ident = (j == i).astype(np.float32)
    tri = (j <= i).astype(np.float32)
    maski = np.where(j <= i, 0.0, NEG).astype(np.float32)
    masks = np.where(j < i, 0.0, NEG).astype(np.float32)
    maskp = np.where(j <= i, -NEG, 0.0).astype(np.float32)
    return np.concatenate([ident, tri, maski, masks, maskp], axis=1)


MH = 8
C1_2PI = 6.28125
C2_2PI = 2.0 * np.pi - 6.28125


def emit_mla(nc, kb, D, ntile=8):
    S = ntile * TTOK
    NW = 1408 + 1024
    with contextlib.ExitStack() as st_:
        kb.tstack = st_
        kb.pfx = D.pfx
        ring = PsumRing(kb, 4)
        acc = [kb.psum(f"acc{i}", [128, 512], F32) for i in range(4)]
        b_acc = kb.bufs(4, "acc")
        for b in b_acc:
            b.excl = True
        ones = kb.sbuf("ones", [128, 128], F32); b_ones = kb.buf("ones")
        MEMSET(kb, "pool", ones[:], 1.0, [b_ones])
        onesb = kb.sbuf("onesb", [128, 128], BF16); b_onesb = kb.buf("onesb")
        MEMSET(kb, "pool", onesb[:], 1.0, [b_onesb])
        cst = kb.sbuf("cst", [128, 512], F32); b_cst = kb.buf("cst")
        kb.dma("sp", cst[:], D.cst_d, writes=[b_cst])
        cstb = kb.sbuf("cstb", [128, 384], BF16); b_cstb = kb.buf("cstb")
        CP(kb, "dve", cstb[:], cst[:, 0:384], [b_cst], [b_cstb])
        identb, foldb, masktri = cstb[:, 0:128], cstb[:, 128:256], cstb[:, 256:384]
        inv_col = cst[:, 384:385]
        sgn_col = cst[:, 385:386]
        phs_col = cst[:, 386:387]
        gsb = kb.sbuf("gsb", [128, 8], F32); b_g = kb.buf("g")
        kb.dma("sp", gsb[:], D.g_d, writes=[b_g])
        gq = kb.sbuf("gq", [128, 6], F32); b_gq = kb.buf("gq")
        kb.dma("sp", gq[:], D.gq_d, writes=[b_gq])
        gkv = kb.sbuf("gkv", [128, 4], F32); b_gkv = kb.buf("gkv")
        kb.dma("sp", gkv[:], D.gkv_d, writes=[b_gkv])

        arena = kb.sbuf("arena", [128, 8 * NW], BF16); b_ar = kb.bufs(2, "arena")
        Win = arena[:, :].rearrange("p (kc n) -> p kc n", kc=8)
        Wuq = kb.sbuf("Wuq", [128, 6, MH * 256], BF16); b_Wuq = kb.buf("Wuq")
        Wukv = kb.sbuf("Wukv", [128, 4, 2048], BF16); b_Wukv = kb.buf("Wukv")
        winv = D.win_d.rearrange("(kc p) n -> p kc n", p=128)
        for kc in range(8):
            kb.dma("pool", Win[:, kc, :], winv[:, kc, :], writes=b_ar)
        wuqv = D.wuq_d.rearrange("(kc p) n -> p kc n", p=128)
        for kc in range(6):
            kb.dma("pool", Wuq[:, kc, :], wuqv[:, kc, :], writes=[b_Wuq])
        kb.dma("pool", Wukv[:], D.wukv_d.rearrange("(kc p) n -> p kc n", p=128), writes=[b_Wukv])

        xt = kb.sbuf("xt", [128, 8, TTOK], F32); b_xt = kb.bufs(8, "xt")
        hT = kb.sbuf("hT", [128, 8, TTOK], BF16); b_h = kb.buf("hT")
        sq_t = [kb.sbuf(f"sq{i}", [128, TTOK], BF16) for i in range(2)]; b_sq = kb.bufs(2, "sq")
        rstd = kb.sbuf("rstd", [128, TTOK], F32); b_rstd = kb.buf("rstd")
        cq32 = kb.sbuf("cq32", [128, 6, TTOK], F32); b_cq32 = kb.bufs(6, "cq32")
        ckv32 = kb.sbuf("ckv32", [128, 4, TTOK], F32); b_ckv32 = kb.bufs(4, "ckv32")
        cqn = kb.sbuf("cqn", [128, 6, TTOK], BF16); b_cqn = kb.buf("cqn")
        ckvn = kb.sbuf("ckvn", [128, 4, TTOK], BF16); b_ckvn = kb.buf("ckvn")
        posi = kb.sbuf("posi", [128, TTOK], I32); b_posi = kb.buf("posi")
        ang = kb.sbuf("ang", [128, TTOK], F32); b_ang = kb.buf("ang")
        kq = kb.sbuf("kq", [128, TTOK], F32); b_kq = kb.buf("kq")
        kqi = kb.sbuf("kqi", [128, TTOK], I32); b_kqi = kb.buf("kqi")
        tabs = kb.sbuf("tabs", [128, TTOK], F32); b_tabs = kb.buf("tabs")
        kprod = kb.sbuf("kprod", [128, TTOK], BF16); b_kprod = kb.buf("kprod")
        krd = kb.sbuf("krd", [128, S], BF16); b_krd = kb.bufs(ntile, "krd")
        stg = [kb.sbuf(f"stg{i}", [128, TTOK], BF16) for i in range(4)]; b_stg = kb.bufs(4, "stg")
        qstg = [kb.sbuf(f"qstg{i}", [128, 2, TTOK], BF16) for i in range(2)]; b_qstg = kb.bufs(2, "qstg")
        vstg = [kb.sbuf(f"vstg{i}", [128, 1024], BF16) for i in range(2)]; b_vstg = kb.bufs(2, "vstg")
        b_qs = [[kb.buf(f"qs{h}_{t}") for t in range(ntile)] for h in range(MH)]
        b_kn = [kb.buf(f"kn{h}") for h in range(MH)]
        b_v = kb.buf("vs")
        b_sg = [[kb.buf(f"sg{h}_{t}") for t in range(ntile)] for h in range(MH)]

        sc = 0
        for T in range(ntile):
            t0 = T * TTOK
            ts_ = slice(t0, t0 + TTOK)
            for kc in range(8):
                kb.dma("sp", xt[:, kc, :], D.xtile(T, kc), reads=D.x_reads, writes=[b_xt[kc]])
            kb.dma("sp", posi[:], D.pos_d[:, ts_], writes=[b_posi])
            rms_stats(kb, ring, onesb[:], b_onesb, sq_t, b_sq, [(xt[:, kc, :], b_xt[kc]) for kc in range(8)],
                      TTOK, rstd[:], b_rstd, 1.0 / 1024)
            for kc in range(8):
                STT(kb, "dve", hT[:, kc, :], xt[:, kc, :], gsb[:, kc:kc + 1], rstd[:], ALU.mult, ALU.mult,
                    [b_xt[kc], b_g, b_rstd], [b_h])
            CP(kb, "dve", ang[:], posi[:], [b_posi], [b_ang])
            TS(kb, "dve", ang[:], ang[:], inv_col, phs_col, ALU.mult, ALU.add, [b_ang, b_cst], [b_ang])
            TS(kb, "dve", kq[:], ang[:], 1.0 / (2 * np.pi), None, ALU.mult, None, [b_ang], [b_kq])
            CP(kb, "dve", kqi[:], kq[:], [b_kq], [b_kqi])
            CP(kb, "dve", kq[:], kqi[:], [b_kqi], [b_kq])
            STT(kb, "dve", ang[:], kq[:], -C1_2PI, ang[:], ALU.mult, ALU.add, [b_kq, b_ang], [b_ang])
            STT(kb, "dve", ang[:], kq[:], -C2_2PI, ang[:], ALU.mult, ALU.add, [b_kq, b_ang], [b_ang])
            TS(kb, "dve", ang[:], ang[:], 3.14159, -3.14159, ALU.min, ALU.max, [b_ang], [b_ang])
            ACT(kb, tabs[:], ang[:], AF.Sin, [b_ang], [b_tabs])
            TS(kb, "dve", tabs[:], tabs[:], sgn_col, None, ALU.mult, None, [b_tabs, b_cst], [b_tabs])
            for oc in range(11):
                ps, bps = ring.next()
                for kc in range(8):
                    MM(kb, ps[:], Win[:, kc, oc * 128:(oc + 1) * 128], hT[:, kc, :], kc == 0, kc == 7, b_ar + [b_h], [bps])
                if oc < 6:
                    CP(kb, "act", cq32[:, oc, :], ps[:], [bps], [b_cq32[oc]])
                elif oc < 10:
                    CP(kb, "act", ckv32[:, oc - 6, :], ps[:], [bps], [b_ckv32[oc - 6]])
                else:
                    TT(kb, "dve", kprod[:], ps[:], tabs[:], ALU.mult, [bps, b_tabs], [b_kprod])
                    ps2, bps2 = ring.next()
                    MM(kb, ps2[:], foldb, kprod[:], True, True, [b_cstb, b_kprod], [bps2])
                    CP(kb, "act", krd[:, ts_], ps2[:], [bps2], [b_krd[T]])
            rms_stats(kb, ring, onesb[:], b_onesb, sq_t, b_sq, [(cq32[:, i, :], b_cq32[i]) for i in range(6)],
                      TTOK, rstd[:], b_rstd, 1.0 / 768)
            for i in range(6):
                STT(kb, "dve", cqn[:, i, :], cq32[:, i, :], gq[:, i:i + 1], rstd[:], ALU.mult, ALU.mult,
                    [b_cq32[i], b_gq, b_rstd], [b_cqn])
            rms_stats(kb, ring, onesb[:], b_onesb, sq_t, b_sq, [(ckv32[:, i, :], b_ckv32[i]) for i in range(4)],
                      TTOK, rstd[:], b_rstd, 1.0 / 512)
            for i in range(4):
                STT(kb, "dve", ckvn[:, i, :], ckv32[:, i, :], gkv[:, i:i + 1], rstd[:], ALU.mult, ALU.mult,
                    [b_ckv32[i], b_gkv, b_rstd], [b_ckvn])
            for h in range(MH):
                ps, bps = ring.next()
                for kc in range(8):
                    MM(kb, ps[:], Win[:, kc, 1408 + h * 128:1408 + (h + 1) * 128], hT[:, kc, :], kc == 0, kc == 7,
                       b_ar + [b_h], [bps])
                s_ = sc % 4; sc += 1
                ACT(kb, stg[s_][:], ps[:], AF.Silu, [bps], [b_stg[s_]])
                kb.dma("sp", D.sg_d[h * 128:(h + 1) * 128, ts_], stg[s_][:], reads=[b_stg[s_]], writes=[b_sg[h][T]])
            for h in range(MH):
                qi = (T * MH + h) % 2
                for part in range(2):
                    ps, bps = ring.next()
                    col = h * 256 + part * 128
                    for kc in range(6):
                        MM(kb, ps[:], Wuq[:, kc, col:col + 128], cqn[:, kc, :], kc == 0, kc == 5, [b_Wuq, b_cqn], [bps])
                    if part == 0:
                        CP(kb, "act", qstg[qi][:, 0, :], ps[:], [bps], [b_qstg[qi]])
                    else:
                        TT(kb, "dve", qstg[qi][:, 1, :], ps[:], tabs[:], ALU.mult, [bps, b_tabs], [b_qstg[qi]])
                kb.dma("sp", D.qs_d[h, :, :, ts_], qstg[qi][:], reads=[b_qstg[qi]], writes=[b_qs[h][T]])
            for h in range(MH):
                ps, bps = ring.next()
                for kc in range(4):
                    MM(kb, ps[:], Wukv[:, kc, h * 128:(h + 1) * 128], ckvn[:, kc, :], kc == 0, kc == 3, [b_Wukv, b_ckvn], [bps])
                s_ = sc % 4; sc += 1
                CP(kb, "act", stg[s_][:], ps[:], [bps], [b_stg[s_]])
                kb.dma("sp", D.kn_d[h, :, ts_], stg[s_][:], reads=[b_stg[s_]], writes=[b_kn[h]])
            for blk in range(4):
                vi = (T * 4 + blk) % 2
                for half in range(2):
                    ps, bps = ring.next()
                    for kc in range(4):
                        MM(kb, ps[:], ckvn[:, kc, blk * 128:(blk + 1) * 128], Wukv[:, kc, 1024 + half * 512:1024 + (half + 1) * 512],
                           kc == 0, kc == 3, [b_Wukv, b_ckvn], [bps])
                    CP(kb, "act", vstg[vi][:, half * 512:(half + 1) * 512], ps[:], [bps], [b_vstg[vi]])
                kb.dma("sp", D.v_d[t0 + blk * 128:t0 + (blk + 1) * 128, :], vstg[vi][:], reads=[b_vstg[vi]], writes=[b_v])

        NKT = S // 128
        KV = [arena[:, i * 8192:(i + 1) * 8192] for i in range(2)]
        qt_ = [kb.sbuf(f"qt{i}", [128, 2, TTOK], BF16) for i in range(2)]; b_qt = kb.bufs(2, "qt")
        sgt = [kb.sbuf(f"sgt{i}", [128, TTOK], BF16) for i in range(2)]; b_sgt = kb.bufs(2, "sgt")
        pT = [kb.sbuf(f"pT{i}", [128, TTOK], BF16) for i in range(5)]; b_pT = kb.bufs(5, "pT")
        rcp = kb.sbuf("rcp", [128, TTOK], F32); b_rcp = kb.buf("rcp")
        o32 = kb.sbuf("o32", [128, TTOK], F32); b_o32 = kb.buf("o32")
        ob = [kb.sbuf(f"ob{i}", [128, TTOK], BF16) for i in range(2)]; b_ob = kb.bufs(2, "ob")
        scale = 192.0 ** -0.5
        nb8 = kb.sbuf("nb8", [128, 1], F32); b_nb8 = kb.buf("nb8")
        MEMSET(kb, "pool", nb8[:], -8.0, [b_nb8])
        vvw = D.v_d.rearrange("(kt p) d -> p kt d", p=128)
        DEPTH = 3
        items = []
        for h in range(MH):
            for qt in range(ntile):
                nkt = 4 * qt + 4
                for kt in range(nkt):
                    items.append((h, qt, kt, nkt))
        state = {"pc": 0, "it": -1}
        grp = {}

        def start_head(h):
            kvb = h % 2
            Kh = KV[kvb][:, 0:4096]
            Vh = KV[kvb][:, 4096:8192].rearrange("p (kt d) -> p kt d", d=128)
            kb.dma("sp", Kh[:, 0:S], D.kn_d[h], reads=[b_kn[h]], writes=[b_ar[kvb]])
            for g4 in range(0, NKT, 8):
                kb.dma("sp", Vh[:, g4:g4 + 8, :], vvw[:, g4:g4 + 8, h * 128:(h + 1) * 128], reads=[b_v], writes=[b_ar[kvb]])

        def start_group(h, qt):
            state["it"] += 1
            qi = state["it"] % 2
            q0 = qt * TTOK
            kb.dma("sp", qt_[qi][:], D.qs_d[h, :, :, q0:q0 + TTOK], reads=[b_qs[h][qt]], writes=[b_qt[qi]])
            kb.dma("sp", sgt[qi][:], D.sg_d[h * 128:(h + 1) * 128, q0:q0 + TTOK], reads=[b_sg[h][qt]], writes=[b_sgt[qi]])
            grp[(h, qt)] = qi

        def emit_S(h, qt, kt, nkt):
            if kt == 0:
                if qt == 0:
                    start_head(h)
                start_group(h, qt)
            qi = grp[(h, qt)]
            kvb = h % 2
            Kh = KV[kvb][:, 0:4096]
            j = kt - 4 * qt
            c0 = 0 if j < 0 else j * 128
            ks = slice(kt * 128, (kt + 1) * 128)
            ps, bps = ring.next()
            MM(kb, ps[:, c0:TTOK], Kh[:, ks], qt_[qi][:, 0, c0:TTOK], True, False, [b_ar[kvb], b_qt[qi]], [bps])
            MM(kb, ps[:, c0:TTOK], krd[:, ks], qt_[qi][:, 1, c0:TTOK], False, j < 0, [b_krd[kt // 4], b_qt[qi]], [bps])
            if j >= 0:
                MM(kb, ps[:, c0:c0 + 128], identb, masktri, False, True, [b_cstb], [bps])
            pi_ = state["pc"] % len(pT); state["pc"] += 1
            ACT(kb, pT[pi_][:, c0:TTOK], ps[:, c0:TTOK], AF.Exp, [bps, b_nb8], [b_pT[pi_]], bias=nb8[:, 0:1], scale=scale)
            return (h, qt, kt, nkt, pi_, c0)

        def emit_PV(h, qt, kt, nkt, pi_, c0):
            qi = grp[(h, qt)]
            kvb = h % 2
            Vh = KV[kvb][:, 4096:8192].rearrange("p (kt d) -> p kt d", d=128)
            ao, asum = acc[qi * 2], acc[qi * 2 + 1]
            bao, basum = b_acc[qi * 2], b_acc[qi * 2 + 1]
            MM(kb, ao[:, c0:TTOK], Vh[:, kt, :], pT[pi_][:, c0:TTOK], kt == 0, kt == nkt - 1, [b_ar[kvb], b_pT[pi_]], [bao])
            MM(kb, asum[:, c0:TTOK], onesb[:], pT[pi_][:, c0:TTOK], kt == 0, kt == nkt - 1, [b_onesb, b_pT[pi_]], [basum])
            if kt == nkt - 1:
                RECIP(kb, rcp[:], asum[:], [basum], [b_rcp])
                TT(kb, "dve", o32[:], ao[:], rcp[:], ALU.mult, [bao, b_rcp], [b_o32])
                TT(kb, "dve", ob[qi][:], o32[:], sgt[qi][:], ALU.mult, [b_o32, b_sgt[qi]], [b_ob[qi]])
                kb.dma("sp", D.out_h[qt // 4][qt % 4][h * 128:(h + 1) * 128, :], ob[qi][:],
                       reads=[b_ob[qi]], writes=[D.b_out[qt // 4][qt % 4]])

        pending = []
        for itx in items:
            pending.append(emit_S(*itx))
            if len(pending) > DEPTH:
                emit_PV(*pending.pop(0))
        while pending:
            emit_PV(*pending.pop(0))
        kb.barrier()
        kb.flush()
    kb.tstack = kb.stack


def _mla_consts():
    j = np.arange(128)[:, None]
    i = np.arange(128)[None, :]
    ident = (j == i).astype(np.float32)
    fold = ((j % 64) == (i % 64)).astype(np.float32)
    masktri = np.where(j <= i, 0.0, NEG).astype(np.float32)
    last = np.zeros((128, 128), np.float32)
    p = np.arange(128)
    last[:, 0] = (10000.0 ** (-(p % 32).astype(np.float64) / 32.0)).astype(np.float32)
    last[:, 1] = np.where(p < 64, 1.0, np.where(p < 96, -1.0, 1.0))
    last[:, 2] = np.where(p < 64, np.pi / 2, 0.0)
    return np.concatenate([ident, fold, masktri, last], axis=1).astype(np.float32)


PAIRS = [[0, 1], [2, 3], [4, 5], [6, 7]]


def build_mega():
    nc = bass.Bass("TRN2", target_bir_lowering=False)
    W = NT + HAL

    def ext(name, shape, dtype=F32):
        return nc.dram_tensor(name, list(shape), dtype, kind="ExternalInput").ap()

    def scr(name, shape, dtype):
        return nc.dram_tensor(name, list(shape), dtype).ap()

    xT_d = ext("xT", [1024, W])
    flag_d = ext("flag", [128, 2])
    invc_d = ext("invcnt", [128, 64])
    P = []
    for i in range(2):
        p = NS()
        p.g_d = ext(f"p{i}_g", [128, 8]); p.win_d = ext(f"p{i}_w_in", [1024, 4096]); p.wgrp_d = ext(f"p{i}_w_grp", [4, 512, 512])
        p.wout_d = ext(f"p{i}_w_out", [2048, 1024]); p.scale_d = ext(f"p{i}_scale", [128, 16])
        P.append(p)
    fg_d = ext("fg", [128, 8])
    G = NS()
    G.pfx = "s2_"
    G.g_d = ext("gd_g", [128, 8]); G.wf_d = ext("gd_wf", [1024, GH * 512]); G.wg_d = ext("gd_wg", [1024, GH * 256])
    G.wba_d = ext("gd_wba", [1024, 8]); G.cw_d = ext("gd_convw", [128, 64]); G.hp_d = ext("gd_hp", [128, 8])
    G.ng_d = ext("gd_ng", [128, 256]); G.cst_d = ext("gd_cst", [128, 5 * 128])
    gd_wout = ext("gd_w_out", [2048, 1024])
    M = NS()
    M.pfx = "s4_"
    M.g_d = ext("ml_g", [128, 8]); M.win_d = ext("ml_w_in", [1024, 1408 + 1024]); M.wuq_d = ext("ml_w_uq", [768, MH * 256])
    M.wukv_d = ext("ml_w_ukv", [512, 2048]); M.gq_d = ext("ml_gq", [128, 6]); M.gkv_d = ext("ml_gkv", [128, 4])
    M.pos_d = ext("ml_pos", [128, SEQ], I32); M.cst_d = ext("ml_cst", [128, 4 * 128])
    ml_wout = ext("ml_w_out", [2048, 1024])
    out_d = nc.dram_tensor("out", [1024, NT], F32, kind="ExternalOutput").ap()

    x1t = [scr(f"x1t{t}", [1024, TTOK], F32) for t in range(4)]; x1g = [scr(f"x1g{t}", [2048, TTOK], F32) for t in range(4)]
    x2t = [scr(f"x2t{t}", [1024, TTOK], F32) for t in range(4)]; x2g = [scr(f"x2g{t}", [2048, TTOK], F32) for t in range(4)]
    ogh = [[scr(f"ogh{i}_{t}", [1024, TTOK], BF16) for t in range(4)] for i in range(2)]
    ogg = [[scr(f"ogg{i}_{t}", [2048, TTOK], BF16) for t in range(4)] for i in range(2)]
    oah = [[scr(f"oah{i}_{t}", [1024, TTOK], BF16) for t in range(4)] for i in range(2)]
    oag = [[scr(f"oag{i}_{t}", [2048, TTOK], BF16) for t in range(4)] for i in range(2)]
    M.qs_d = scr("ml_qs", [MH, 128, 2, SEQ], BF16); M.kn_d = scr("ml_kn", [MH, 128, SEQ], BF16)
    M.v_d = scr("ml_vs", [SEQ, MH * 128], BF16); M.sg_d = scr("ml_sgs", [MH * 128, SEQ], BF16)

    with contextlib.ExitStack() as st:
        kb = KB(nc, st)
        b_x1t, b_x1g, b_x2t, b_x2g = kb.bufs(4, "x1t"), kb.bufs(4, "x1g"), kb.bufs(4, "x2t"), kb.bufs(4, "x2g")
        b_ogh = [kb.bufs(4, f"ogh{i}_") for i in range(2)]; b_ogg = [kb.bufs(4, f"ogg{i}_") for i in range(2)]
        b_oah = [kb.bufs(4, f"oah{i}_") for i in range(2)]; b_oag = [kb.bufs(4, f"oag{i}_") for i in range(2)]

        def xtile_from(xg):
            return lambda T, kc: xg[T % 4][(T // 4) * 1024 + kc * 128:(T // 4) * 1024 + (kc + 1) * 128, :]

        D = NS(); D.pfx = "s1_"; D.flag_d = flag_d
        D.x_main = [xT_d[:, HAL + t * TTOK:HAL + (t + 1) * TTOK] for t in range(4)]; D.x_reads = []
        D.x_halo = xT_d[:, 0:HAL]; D.xh_reads = []; D.halo_flag = False
        D.g_d, D.win_d, D.wgrp_d, D.wout_d, D.scale_d, D.invc_d = P[0].g_d, P[0].win_d, P[0].wgrp_d, P[0].wout_d, P[0].scale_d, invc_d
        D.out = x1t; D.out_is_output = False; D.b_out = b_x1t
        D.after_tile = lambda t: kb.allgather(x1g[t], x1t[t], PAIRS, reads=[b_x1t[t]], writes=[b_x1g[t]])
        emit_tok(nc, kb, D, False, True, False)
        G.xtile = xtile_from(x1g); G.x_reads = b_x1g; G.out_h = ogh; G.b_out = b_ogh
        G.after_tile = lambda T: kb.allgather(ogg[T // 4][T % 4], ogh[T // 4][T % 4], PAIRS,
                                              reads=[b_ogh[T // 4][T % 4]], writes=[b_ogg[T // 4][T % 4]])
        emit_gdn(nc, kb, G, SEQ // TTOK)
        D = NS(); D.pfx = "s3_"; D.flag_d = flag_d
        D.x_main = x1t; D.x_reads = b_x1t; D.x_halo = None; D.halo_flag = False
        D.y0, D.y1, D.y_reads, D.y_halo = ogg[0], ogg[1], b_ogg[0] + b_ogg[1], None
        D.wprev_d = gd_wout
        D.out = x2t; D.out_is_output = False; D.b_out = b_x2t
        D.after_tile = lambda t: kb.allgather(x2g[t], x2t[t], PAIRS, reads=[b_x2t[t]], writes=[b_x2g[t]])
        emit_tok(nc, kb, D, True, False, False)
        M.xtile = xtile_from(x2g); M.x_reads = b_x2g; M.out_h = oah; M.b_out = b_oah
        emit_mla(nc, kb, M, SEQ // TTOK)
        for i in range(2):
            for t in range(4):
                kb.allgather(oag[i][t], oah[i][t], PAIRS, reads=[b_oah[i][t]], writes=[b_oag[i][t]])
        D = NS(); D.pfx = "s5_"; D.flag_d = flag_d
        D.x_main = x2t; D.x_reads = b_x2t; D.x_halo = x2g[3][0:1024, TTOK - HAL:TTOK]; D.xh_reads = b_x2g; D.halo_flag = True
        D.y0, D.y1, D.y_reads, D.y_halo = oag[0], oag[1], b_oag[0] + b_oag[1], oag[0][3][:, TTOK - HAL:TTOK]
        D.wprev_d = ml_wout
        D.g_d, D.win_d, D.wgrp_d, D.wout_d, D.scale_d, D.invc_d = P[1].g_d, P[1].win_d, P[1].wgrp_d, P[1].wout_d, P[1].scale_d, invc_d
        D.fg_d = fg_d
        D.out = [out_d[:, t * TTOK:(t + 1) * TTOK] for t in range(4)]; D.out_is_output = True; D.b_out = None
        D.after_tile = None
        emit_tok(nc, kb, D, True, True, True)
        kb.finish()
    return nc


_NC = {}


def _gdn_host(c, w_in, conv_w, a_log, dt_bias, norm_g, g):
    hg = c % 2
    heads = [hg * GH + i for i in range(GH)]
    cols, cwl = [], []
    for h in heads:
        for (base, n) in ((h * 128, 128), (1024 + h * 128, 128), (2048 + h * 256, 128), (2048 + h * 256 + 128, 128)):
            cols.extend(range(base, base + n))
            cwl.append(conv_w[:, base:base + n].T)
    gcols = []
    for h in heads:
        gcols.extend(range(4096 + h * 256, 4096 + (h + 1) * 256))
    bacols = [6144 + h for h in heads] + [6152 + h for h in heads]
    hpv = np.concatenate([a_log[heads], dt_bias[heads]]).astype(np.float32)
    return {
        "gd_g": _colmajor128(g, 8),
        "gd_wf": np.ascontiguousarray(w_in[:, cols]),
        "gd_wg": np.ascontiguousarray(w_in[:, gcols]),
        "gd_wba": np.ascontiguousarray(w_in[:, bacols]),
        "gd_convw": np.ascontiguousarray(np.concatenate(cwl, axis=1), dtype=np.float32),
        "gd_hp": np.ascontiguousarray(np.broadcast_to(hpv[None, :], (128, 8))),
        "gd_ng": np.ascontiguousarray(np.broadcast_to(np.asarray(norm_g, np.float32)[None, :], (128, 256))),
        "gd_cst": _gdn_consts(),
    }


def _mla_host(c, positions_b, g, w_in, gq, w_uq, gkv, w_ukv):
    hg = c % 2
    sw = list(range(32, 64)) + list(range(0, 32))
    heads = [hg * MH + i for i in range(MH)]
    kr0 = 768 + 512
    cols = list(range(0, kr0 + 64)) + [kr0 + s_ for s_ in sw]
    for h in heads:
        cols.extend(range(kr0 + 64 + h * 128, kr0 + 64 + (h + 1) * 128))
    qcols = []
    for h in heads:
        base = h * 192
        qcols.extend(range(base, base + 192))
        qcols.extend([base + 128 + s_ for s_ in sw])
    kvcols = []
    for h in heads:
        kvcols.extend(range(h * 256, h * 256 + 128))
    for h in heads:
        kvcols.extend(range(h * 256 + 128, h * 256 + 256))
    return {
        "ml_g": _colmajor128(g, 8),
        "ml_w_in": np.ascontiguousarray(w_in[:, cols]),
        "ml_w_uq": np.ascontiguousarray(w_uq[:, qcols]),
        "ml_w_ukv": np.ascontiguousarray(w_ukv[:, kvcols]),
        "ml_gq": _colmajor128(gq, 6),
        "ml_gkv": _colmajor128(gkv, 4),
        "ml_pos": np.ascontiguousarray(np.broadcast_to(positions_b.astype(np.int32)[None, :], (128, SEQ))),
        "ml_cst": _mla_consts(),
    }


def kernel(x, positions, norm_g, pool_w_in, pool_w_grp, pool_scale, pool_w_out,
           gdn_w_in, gdn_conv, gdn_a_log, gdn_dt_bias, gdn_norm_g, gdn_w_out,
           mla_w_in, mla_q_norm_g, mla_w_uq, mla_kv_norm_g, mla_w_ukv, mla_w_out, final_g):
    f = lambda a: np.ascontiguousarray(np.asarray(a, dtype=np.float32))
    x = f(x); norm_g = f(norm_g); positions = np.asarray(positions)
    pool_w_in, pool_w_grp, pool_scale, pool_w_out = f(pool_w_in), f(pool_w_grp), f(pool_scale), f(pool_w_out)
    gdn_w_in, gdn_conv = f(gdn_w_in)[0], f(gdn_conv)[0]
    mla_w_in, mla_w_uq, mla_w_ukv = f(mla_w_in)[0], f(mla_w_uq)[0], f(mla_w_ukv)[0]
    if "mega" not in _NC:
        _NC["mega"] = build_mega()
    nc = _NC["mega"]
    in_maps = []
    for c in range(NCORES):
        b, r = c // 2, c % 2
        m = {"xT": _halo_T(x[b], r),
             "flag": np.ascontiguousarray(np.broadcast_to(np.array([[float(r), 1.0 - float(r)]], np.float32), (128, 2))),
             "invcnt": _invcnt(r)}
        for i, li in enumerate((0, 3)):
            m[f"p{i}_g"] = _colmajor128(norm_g[li], 8)
            m[f"p{i}_w_in"] = pool_w_in[i]
            m[f"p{i}_w_grp"] = pool_w_grp[i]
            m[f"p{i}_w_out"] = pool_w_out[i]
            m[f"p{i}_scale"] = _colmajor128(pool_scale[i], 16)
        m["fg"] = _colmajor128(f(final_g), 8)
        m.update(_gdn_host(c, gdn_w_in, gdn_conv, f(gdn_a_log)[0], f(gdn_dt_bias)[0], f(gdn_norm_g)[0], norm_g[1]))
        m["gd_w_out"] = f(gdn_w_out)[0]
        m.update(_mla_host(c, positions[b], norm_g[2], mla_w_in, f(mla_q_norm_g)[0], mla_w_uq, f(mla_kv_norm_g)[0], mla_w_ukv))
        m["ml_w_out"] = f(mla_w_out)[0]
        in_maps.append(m)
    res = run_bass_kernel_spmd(nc, in_maps, core_ids=list(range(NCORES)))
    out = np.empty(x.shape, np.float32)
    for c in range(NCORES):
        b, r = c // 2, c % 2
        out[b, r * NT:(r + 1) * NT] = res.results[c]["out"].T
    return out
```
